# Optimizing a Trainium2 kernel written in Bass

```python
import math
import jax
import jax.numpy as jnp
from jax import lax
import numpy as np

D_MODEL = 1024
BATCH = 8
SEQ = 8192
DEPTH = 4

N_META = 16
CHUNK = 64
PAD_FRONT = CHUNK - N_META
N_BRANCH = 4
W_MIX = D_MODEL // 4
HEAD_DIM = 64
N_HEADS = W_MIX // HEAD_DIM
MLSTM_CONV = 4
RANK_W = 64
RANK_A = 64
RANK_V = 32
RANK_G = 128
S5_GROUP = 16
S5_GROUPS = W_MIX // S5_GROUP
S5_STATE = 64
D_FF = 256 * ((8 * D_MODEL // 3 + 255) // 256)
FFN_CONV = 3
M_COLS = 4 * W_MIX + 2 * N_HEADS
R_COLS = 3 * W_MIX + RANK_W + RANK_A + RANK_G
H_COLS = 4 * W_MIX
S_COLS = W_MIX
IN_COLS = M_COLS + R_COLS + H_COLS + S_COLS
RMS_EPS = 1e-6
GN_EPS = 64e-5
L2_EPS = 1e-12
LB_FLOOR = 1e-30
NEG = -1e30
F32 = jnp.float32

kernel_name = 'hybrid_mlstm_rwkv7_hgrn2_s5_trunk'


def _split(a, sizes):
    return jnp.split(a, np.cumsum(sizes)[:-1].tolist(), axis=-1)


def _rmsnorm(x, g):
    xf = x.astype(F32)
    y = xf * lax.rsqrt(jnp.mean(xf * xf, axis=-1, keepdims=True) + RMS_EPS)
    return (y * g.astype(F32)).astype(x.dtype)


def _head_rmsnorm(y, g):
    y = y * lax.rsqrt(jnp.mean(y * y, axis=-1, keepdims=True) + RMS_EPS)
    return y * g.astype(F32).reshape(N_HEADS, HEAD_DIM)


def _causal_dwconv(x, w, b):
    k, c = w.shape
    y = lax.conv_general_dilated(x, w.astype(x.dtype)[:, None, :], window_strides=(1,),
                                 padding=[(k - 1, 0)], dimension_numbers=('NWC', 'WIO', 'NWC'),
                                 feature_group_count=c)
    return y + b.astype(x.dtype)


def _heads(a):
    b, t, _ = a.shape
    return a.astype(F32).reshape(b, t, N_HEADS, HEAD_DIM).transpose(0, 2, 1, 3)


def _pad_front(a, value):
    pad = [(0, 0)] * a.ndim
    pad[2] = (PAD_FRONT, 0)
    return jnp.pad(a, pad, constant_values=value)


def _to_chunks(a):
    b, hh, l = a.shape[:3]
    a = a.reshape((b, hh, l // CHUNK, CHUNK) + a.shape[3:])
    return jnp.moveaxis(a, 2, 0)


def _from_chunks(a):
    a = jnp.moveaxis(a, 0, 2)
    a = a.reshape(a.shape[:2] + (a.shape[2] * a.shape[3],) + a.shape[4:])
    return a[:, :, PAD_FRONT:]


def _causal_mask():
    return jnp.tril(jnp.ones((CHUNK, CHUNK), dtype=bool))


def _mlstm_chunk(carry, xs):
    c_st, n_st, m_st = carry
    q, k, v, li, lf = xs
    b = jnp.cumsum(lf, axis=-1)
    dmat = jnp.where(_causal_mask(), b[..., :, None] - b[..., None, :] + li[..., None, :], NEG)
    inter = b + m_st[..., None]
    m = jnp.maximum(inter, jnp.max(dmat, axis=-1))
    wmat = jnp.exp(dmat - m[..., None])
    sc = jnp.exp(inter - m)
    s = jnp.einsum('bhtd,bhsd->bhts', q, k) * wmat
    num = sc[..., None] * jnp.einsum('bhtd,bhde->bhte', q, c_st) + jnp.einsum('bhts,bhse->bhte', s, v)
    den = sc * jnp.einsum('bhtd,bhd->bht', q, n_st) + jnp.sum(s, axis=-1)
    h = num / jnp.maximum(jnp.abs(den), jnp.exp(-m))[..., None]
    b_last = b[..., -1]
    g = b_last[..., None] - b + li
    m_new = jnp.maximum(b_last + m_st, jnp.max(g, axis=-1))
    ws = jnp.exp(g - m_new[..., None])
    dec = jnp.exp(b_last + m_st - m_new)
    c_new = dec[..., None, None] * c_st + jnp.einsum('bhsd,bhse->bhde', k * ws[..., None], v)
    n_new = dec[..., None] * n_st + jnp.einsum('bhs,bhsd->bhd', ws, k)
    return (c_new, n_new, m_new), h


def _mlstm(p, conv_w, conv_b, gate_b, norm_g):
    dt = p.dtype
    bsz, t, _ = p.shape
    q, k, v, o, ig, fg = _split(p, [W_MIX] * 4 + [N_HEADS] * 2)
    qk = jax.nn.silu(_causal_dwconv(jnp.concatenate([q, k], axis=-1), conv_w, conv_b))
    q, k = jnp.split(qk, 2, axis=-1)
    log_i = (ig + gate_b[0]).astype(F32).transpose(0, 2, 1)
    log_f = jax.nn.log_sigmoid((fg + gate_b[1]).astype(F32)).transpose(0, 2, 1)
    qh = _heads(q) * (HEAD_DIM ** -0.5)
    kh = _heads(k)
    vh = _heads(v)
    xs = (_to_chunks(_pad_front(qh, 0.0)), _to_chunks(_pad_front(kh, 0.0)), _to_chunks(_pad_front(vh, 0.0)),
          _to_chunks(_pad_front(log_i, NEG)), _to_chunks(_pad_front(log_f, 0.0)))
    init = (jnp.zeros((bsz, N_HEADS, HEAD_DIM, HEAD_DIM), F32), jnp.zeros((bsz, N_HEADS, HEAD_DIM), F32),
            jnp.zeros((bsz, N_HEADS), F32))
    _, hs = lax.scan(_mlstm_chunk, init, xs)
    hs = _from_chunks(hs).transpose(0, 2, 1, 3)
    y = _head_rmsnorm(hs, norm_g).reshape(bsz, t, W_MIX)
    return (jax.nn.sigmoid(o.astype(F32)) * y).astype(dt)


def _rwkv7_step(s_st, xs):
    r, w, k, v, a, b = xs
    sa = jnp.einsum('bhij,bhj->bhi', s_st, a)
    s_st = s_st * w[:, :, None, :] + sa[..., None] * b[:, :, None, :] + v[..., None] * k[:, :, None, :]
    return s_st, jnp.einsum('bhij,bhj->bhi', s_st, r)


def _rwkv7(p, mu, w0, w2, a0, a2, g2, k_k, k_a, r_k, gn_w, gn_b, v_first, v_res):
    dt = p.dtype
    bsz, t, _ = p.shape
    shifted = jnp.pad(p[:, :-1], ((0, 0), (1, 0), (0, 0)))
    xm = p + (shifted - p) * mu
    r, k, v, cw, ca, cg = _split(xm, [W_MIX] * 3 + [RANK_W, RANK_A, RANK_G])
    w = -jax.nn.softplus(-(w0 + jnp.tanh(cw) @ w2)) - 0.5
    if v_res is None:
        v_first = v
    else:
        v0, v1, v2 = v_res
        v = v + (v_first - v) * jax.nn.sigmoid(v0 + (v @ v1) @ v2)
    a = jax.nn.sigmoid(a0 + ca @ a2)
    g = jax.nn.sigmoid(cg) @ g2

    def hd(z):
        return z.astype(F32).reshape(bsz, t, N_HEADS, HEAD_DIM)

    kk = hd(k * k_k)
    kk = kk / jnp.maximum(jnp.sqrt(jnp.sum(kk * kk, axis=-1, keepdims=True)), L2_EPS)
    k = k * (1.0 + (a - 1.0) * k_a)
    rh, kh, vh, ah = hd(r), hd(k), hd(v), hd(a)
    decay = jnp.exp(-jnp.exp(hd(w)))

    def tm(z):
        return jnp.moveaxis(z, 1, 0)

    xs = (tm(rh), tm(decay), tm(kh), tm(vh), tm(-kk), tm(kk * ah))
    _, y = lax.scan(_rwkv7_step, jnp.zeros((bsz, N_HEADS, HEAD_DIM, HEAD_DIM), F32), xs)
    y = jnp.moveaxis(y, 0, 1)
    mean = jnp.mean(y, axis=-1, keepdims=True)
    var = jnp.mean(jnp.square(y - mean), axis=-1, keepdims=True)
    y = (y - mean) * lax.rsqrt(var + GN_EPS) * gn_w.astype(F32).reshape(N_HEADS, HEAD_DIM) \
        + gn_b.astype(F32).reshape(N_HEADS, HEAD_DIM)
    bonus = jnp.sum(rh * kh * r_k.astype(F32).reshape(N_HEADS, HEAD_DIM), axis=-1, keepdims=True)
    y = (y + bonus * vh).reshape(bsz, t, W_MIX) * g.astype(F32)
    return y.astype(dt), v_first


def _hgrn2_chunk(s_st, xs):
    q, k, v, lf = xs
    g = jnp.cumsum(lf, axis=2)
    o_inter = jnp.einsum('bhtd,bhde->bhte', q * jnp.exp(g), s_st)
    diff = jnp.where(_causal_mask()[:, :, None], g[:, :, :, None, :] - g[:, :, None, :, :], NEG)
    att = jnp.sum(q[:, :, :, None, :] * k[:, :, None, :, :] * jnp.exp(diff), axis=-1)
    o = o_inter + jnp.einsum('bhts,bhse->bhte', att, v)
    g_last = g[:, :, -1:, :]
    s_new = jnp.exp(g_last[:, :, 0, :, None]) * s_st + jnp.einsum('bhsd,bhse->bhde', k * jnp.exp(g_last - g), v)
    return s_new, o


def _hgrn2(p, lb, norm_g):
    dt = p.dtype
    bsz, t, _ = p.shape
    q, f, i, g = _split(p, [W_MIX] * 4)
    z = f.astype(F32)
    lb = lb.astype(F32)
    log_lb = jnp.log(jnp.maximum(lb, LB_FLOOR))
    log_f = jnp.logaddexp(log_lb, jnp.log1p(-lb) + jax.nn.log_sigmoid(z))
    k = (1.0 - lb) * jax.nn.sigmoid(-z)
    xs = (_to_chunks(_pad_front(_heads(jax.nn.silu(q)), 0.0)), _to_chunks(_pad_front(_heads(k), 0.0)),
          _to_chunks(_pad_front(_heads(i), 0.0)), _to_chunks(_pad_front(_heads(log_f), 0.0)))
    _, o = lax.scan(_hgrn2_chunk, jnp.zeros((bsz, N_HEADS, HEAD_DIM, HEAD_DIM), F32), xs)
    o = _from_chunks(o).transpose(0, 2, 1, 3)
    y = _head_rmsnorm(o, norm_g).reshape(bsz, t, W_MIX)
    return (y * jax.nn.silu(g.astype(F32))).astype(dt)


def _s5_combine(e1, e2):
    a1r, a1i, b1r, b1i = e1
    a2r, a2i, b2r, b2i = e2
    return (a2r * a1r - a2i * a1i, a2r * a1i + a2i * a1r,
            a2r * b1r - a2i * b1i + b2r, a2r * b1i + a2i * b1r + b2i)


def _s5(u, a_re, a_im, b_re, b_im, c_re, c_im, d_skip, log_step, w_glu, b_glu):
    dt = u.dtype
    bsz, t, _ = u.shape
    uf = u.astype(F32)
    ug = uf.reshape(bsz, t, S5_GROUPS, S5_GROUP)
    a_re = a_re.astype(F32)
    a_im = a_im.astype(F32)
    step = jnp.exp(log_step.astype(F32))[:, None]
    mag = jnp.exp(a_re * step)
    ang = a_im * step
    ab_re = mag * jnp.cos(ang)
    ab_im = mag * jnp.sin(ang)
    den = a_re * a_re + a_im * a_im
    num_re = ab_re - 1.0
    coef_re = (num_re * a_re + ab_im * a_im) / den
    coef_im = (ab_im * a_re - num_re * a_im) / den
    b_re = b_re.astype(F32)
    b_im = b_im.astype(F32)
    bb_re = coef_re[..., None] * b_re - coef_im[..., None] * b_im
    bb_im = coef_re[..., None] * b_im + coef_im[..., None] * b_re
    bu_re = jnp.einsum('btgc,gpc->btgp', ug, bb_re)
    bu_im = jnp.einsum('btgc,gpc->btgp', ug, bb_im)
    shape = (1, t, S5_GROUPS, S5_STATE)
    elems = (jnp.broadcast_to(ab_re, shape), jnp.broadcast_to(ab_im, shape), bu_re, bu_im)
    _, _, s_re, s_im = lax.associative_scan(_s5_combine, elems, axis=1)
    y = jnp.einsum('btgp,gcp->btgc', s_re, c_re.astype(F32)) - jnp.einsum('btgp,gcp->btgc', s_im, c_im.astype(F32))
    y = y.reshape(bsz, t, W_MIX) + d_skip.astype(F32) * uf
    y = jax.nn.gelu(y)
    y = y * jax.nn.sigmoid(y @ w_glu.astype(F32) + b_glu.astype(F32))
    return y.astype(dt)


def _conv_ffn(u, w_up, conv_w, conv_b, w_down):
    z = _causal_dwconv(u @ w_up, conv_w, conv_b)
    gate, val = jnp.split(z, 2, axis=-1)
    return (jax.nn.gelu(gate) * val) @ w_down


def setup_inputs(seed: int = 0) -> dict:
    key = jax.random.key(seed)
    keys = iter(jax.random.split(key, 64))
    L = DEPTH

    def nrm(shape, scale):
        return scale * jax.random.normal(next(keys), shape, F32)

    x = nrm((BATCH, SEQ, D_MODEL), 1.0)
    meta = nrm((N_META, D_MODEL), 1.0)
    norms = 1.0 + nrm((L, 4, D_MODEL), 0.02)
    w_in = nrm((L, D_MODEL, IN_COLS), D_MODEL ** -0.5)
    w_gate = nrm((L, N_BRANCH, D_MODEL, D_MODEL), D_MODEL ** -0.5)
    b_gate = nrm((L, N_BRANCH, D_MODEL), 0.02)
    w_branch = nrm((L, N_BRANCH, W_MIX, D_MODEL), W_MIX ** -0.5)
    w_out = nrm((L, D_MODEL, D_MODEL), D_MODEL ** -0.5)
    m_conv_w = nrm((L, MLSTM_CONV, 2 * W_MIX), MLSTM_CONV ** -0.5)
    m_conv_b = nrm((L, 2 * W_MIX), 0.02)
    f_bias = jnp.linspace(3.0, 6.0, N_HEADS, dtype=F32)
    m_gate_b = jnp.stack([nrm((L, N_HEADS), 0.1), f_bias + nrm((L, N_HEADS), 0.1)], axis=1)
    m_norm = 1.0 + nrm((L, W_MIX), 0.02)
    r_mu = jax.random.uniform(next(keys), (L, R_COLS), F32)
    r_w0 = jnp.linspace(-6.0, -1.0, W_MIX, dtype=F32) + nrm((L, W_MIX), 0.1)
    r_w2 = nrm((L, RANK_W, W_MIX), 0.5 * RANK_W ** -0.5)
    r_a0 = nrm((L, W_MIX), 0.1)
    r_a2 = nrm((L, RANK_A, W_MIX), 0.5 * RANK_A ** -0.5)
    r_g2 = nrm((L, RANK_G, W_MIX), RANK_G ** -0.5)
    r_kk = 0.85 + nrm((L, W_MIX), 0.02)
    r_ka = 1.0 + nrm((L, W_MIX), 0.02)
    r_rk = nrm((L, W_MIX), 0.1)
    r_gn_w = 1.0 + nrm((L, W_MIX), 0.02)
    r_gn_b = nrm((L, W_MIX), 0.02)
    r_v0 = 1.0 + nrm((L - 1, W_MIX), 0.1)
    r_v1 = nrm((L - 1, W_MIX, RANK_V), W_MIX ** -0.5)
    r_v2 = nrm((L - 1, RANK_V, W_MIX), 0.5 * RANK_V ** -0.5)
    h_lb = 1.0 + nrm((L, W_MIX), 0.1)
    h_norm = 1.0 + nrm((L, W_MIX), 0.02)
    s_a_re = -0.5 + nrm((L, S5_GROUPS, S5_STATE), 0.01)
    s_a_im = math.pi * jnp.arange(S5_STATE, dtype=F32) + nrm((L, S5_GROUPS, S5_STATE), 0.01)
    s_b_re = nrm((L, S5_GROUPS, S5_STATE, S5_GROUP), (2 * S5_GROUP) ** -0.5)
    s_b_im = nrm((L, S5_GROUPS, S5_STATE, S5_GROUP), (2 * S5_GROUP) ** -0.5)
    s_c_re = nrm((L, S5_GROUPS, S5_GROUP, S5_STATE), S5_STATE ** -0.5)
    s_c_im = nrm((L, S5_GROUPS, S5_GROUP, S5_STATE), S5_STATE ** -0.5)
    s_d = nrm((L, W_MIX), 1.0)
    lo, hi = math.log(1e-3), math.log(1e-1)
    s_log_step = lo + (hi - lo) * jax.random.uniform(next(keys), (L, S5_GROUPS), F32)
    s_w_glu = nrm((L, W_MIX, W_MIX), W_MIX ** -0.5)
    s_b_glu = nrm((L, W_MIX), 0.02)
    f_up = nrm((L, D_MODEL, 2 * D_FF), D_MODEL ** -0.5)
    f_conv_w = nrm((L, FFN_CONV, 2 * D_FF), FFN_CONV ** -0.5)
    f_conv_b = nrm((L, 2 * D_FF), 0.02)
    f_down = nrm((L, D_FF, D_MODEL), D_FF ** -0.5)
    return {'x': x, 'meta': meta, 'norms': norms, 'w_in': w_in, 'w_gate': w_gate, 'b_gate': b_gate,
            'w_branch': w_branch, 'w_out': w_out, 'm_conv_w': m_conv_w, 'm_conv_b': m_conv_b,
            'm_gate_b': m_gate_b, 'm_norm': m_norm, 'r_mu': r_mu, 'r_w0': r_w0, 'r_w2': r_w2,
            'r_a0': r_a0, 'r_a2': r_a2, 'r_g2': r_g2, 'r_kk': r_kk, 'r_ka': r_ka, 'r_rk': r_rk,
            'r_gn_w': r_gn_w, 'r_gn_b': r_gn_b, 'r_v0': r_v0, 'r_v1': r_v1, 'r_v2': r_v2,
            'h_lb': h_lb, 'h_norm': h_norm, 's_a_re': s_a_re, 's_a_im': s_a_im, 's_b_re': s_b_re,
            's_b_im': s_b_im, 's_c_re': s_c_re, 's_c_im': s_c_im, 's_d': s_d, 's_log_step': s_log_step,
            's_w_glu': s_w_glu, 's_b_glu': s_b_glu, 'f_up': f_up, 'f_conv_w': f_conv_w,
            'f_conv_b': f_conv_b, 'f_down': f_down}


def reference(x, meta, norms, w_in, w_gate, b_gate, w_branch, w_out, m_conv_w, m_conv_b, m_gate_b, m_norm,
              r_mu, r_w0, r_w2, r_a0, r_a2, r_g2, r_kk, r_ka, r_rk, r_gn_w, r_gn_b, r_v0, r_v1, r_v2,
              h_lb, h_norm, s_a_re, s_a_im, s_b_re, s_b_im, s_c_re, s_c_im, s_d, s_log_step, s_w_glu, s_b_glu,
              f_up, f_conv_w, f_conv_b, f_down):
    dt = x.dtype
    bsz = x.shape[0]
    h = jnp.concatenate([jnp.broadcast_to(meta.astype(dt)[None], (bsz, N_META, D_MODEL)), x], axis=1)
    lb_w = jax.nn.softmax(h_lb.astype(F32), axis=0)
    lbs = jnp.cumsum(lb_w, axis=0) - lb_w[0:1]
    v_first = None
    for l in range(DEPTH):
        u = _rmsnorm(h, norms[l, 0])
        p_m, p_r, p_h, p_s = _split(u @ w_in[l], [M_COLS, R_COLS, H_COLS, S_COLS])
        y_m = _mlstm(p_m, m_conv_w[l], m_conv_b[l], m_gate_b[l], m_norm[l])
        v_res = None if l == 0 else (r_v0[l - 1], r_v1[l - 1], r_v2[l - 1])
        y_r, v_first = _rwkv7(p_r, r_mu[l], r_w0[l], r_w2[l], r_a0[l], r_a2[l], r_g2[l], r_kk[l], r_ka[l],
                              r_rk[l], r_gn_w[l], r_gn_b[l], v_first, v_res)
        y_h = _hgrn2(p_h, lbs[l], h_norm[l])
        y_s = _s5(p_s, s_a_re[l], s_a_im[l], s_b_re[l], s_b_im[l], s_c_re[l], s_c_im[l], s_d[l],
                  s_log_step[l], s_w_glu[l], s_b_glu[l])
        ys = (y_m, y_r, y_h, y_s)
        merged = None
        for n_b in range(N_BRANCH):
            gate = jax.nn.sigmoid(u @ w_gate[l, n_b] + b_gate[l, n_b])
            term = gate * (ys[n_b] @ w_branch[l, n_b])
            merged = term if merged is None else merged + term
        h = h + _rmsnorm(merged @ w_out[l], norms[l, 1])
        u2 = _rmsnorm(h, norms[l, 2])
        h = h + _rmsnorm(_conv_ffn(u2, f_up[l], f_conv_w[l], f_conv_b[l], f_down[l]), norms[l, 3])
    return h[:, N_META:]
```

```python
import math
import numpy as np
import ml_dtypes
import concourse.bass as bass
import concourse.mybir as mybir
from concourse.bass_utils import run_bass_kernel_spmd

F32 = mybir.dt.float32
F32R = mybir.dt.float32r
BF16 = mybir.dt.bfloat16
I32 = mybir.dt.int32
ALU = mybir.AluOpType
AF = mybir.ActivationFunctionType

D = 1024
LT = 512
PADF = 496
NMETA = 16
G = 16
NBLK = 37
RMS_EPS = 1e-6
GN_EPS = 64e-5
TWO_PI = 2.0 * math.pi
S5W_ = 256 + 256 + 1024 + 128

import os as _os
NWR = int(_os.environ.get("NWR", "2"))
SAME_ENGINE_SYNC = _os.environ.get("SES", "1") == "1"


class View:
    __slots__ = ("buf", "ap")

    def __init__(self, buf, ap):
        self.buf = buf
        self.ap = ap

    def __getitem__(self, idx):
        return View(self.buf, self.ap[idx])

    def m(self, f):
        return View(self.buf, f(self.ap))

    def re(self, pat, **kw):
        return View(self.buf, self.ap.rearrange(pat, **kw))

    def bc(self, shape):
        return View(self.buf, self.ap.broadcast_to(list(shape)))


class Buf:
    __slots__ = ("t", "name", "w", "r")

    def __init__(self, t, name):
        self.t = t
        self.name = name
        self.w = None
        self.r = {}

    def __getitem__(self, idx):
        return View(self, self.t[idx])


class Eng:
    def __init__(self, fw, name, h):
        self.name = name
        self.h = h
        self.sem = fw.new_sem("e_" + name)
        self.count = 0
        self.known = {}
        self.prog = []
        self.dma_ring = None
        self.dma_last = None
        self.dma_i = 0


def _ap(x):
    return x.ap if isinstance(x, View) else x


def _bufs(xs):
    out = []
    for x in xs:
        if isinstance(x, View):
            out.append(x.buf)
        elif isinstance(x, Buf):
            out.append(x)
    return out


class FW:
    def __init__(self, nc, n_dma_sems=16):
        self.nc = nc
        self.sems = {}
        self.semvals = {}
        self.E = {}
        for name, h in (("pe", nc.tensor), ("dve", nc.vector), ("act", nc.scalar),
                        ("pool", nc.gpsimd), ("sp", nc.sync)):
            self.E[name] = Eng(self, name, h)
        for qn in ("sp", "pool", "act"):
            e = self.E[qn]
            e.dma_ring = [self.new_sem("d_%s_%d" % (qn, i)) for i in range(n_dma_sems)]
            e.dma_last = [None] * n_dma_sems
        self.ninstr = 0

    def new_sem(self, name):
        s = self.nc.alloc_semaphore(name)
        self.sems[name] = s
        self.semvals[name] = 0
        return name

    def sbuf(self, name, shape, dtype=F32):
        return Buf(self.nc.alloc_sbuf_tensor(name, list(shape), dtype), name)

    def psum(self, name, shape, dtype=F32):
        return Buf(self.nc.alloc_psum_tensor(name, list(shape), dtype), name)

    def dram(self, name, shape, dtype=F32, kind="Internal"):
        return Buf(self.nc.dram_tensor(name, list(shape), dtype, kind=kind), name)

    def _collect(self, eng, reads, writes):
        waits = {}

        def add(ev):
            k, v = ev
            if waits.get(k, 0) < v:
                waits[k] = v
        for b in reads:
            if b.w is not None:
                if b.w[0] == eng.sem and not SAME_ENGINE_SYNC:
                    continue
                add(b.w)
        for b in writes:
            if b.w is not None and b.w[0] != eng.sem:
                add(b.w)
            for k, v in b.r.items():
                if k != eng.sem:
                    add((k, v))
        out = []
        for k, v in waits.items():
            if eng.known.get(k, 0) >= v:
                continue
            eng.known[k] = v
            out.append((k, v))
        return out

    def _emit_waits(self, eng, waits):
        for k, v in waits:
            s = self.sems[k]
            eng.prog.append(lambda h=eng.h, s=s, v=v: h.wait_ge(s, v))

    def _mark(self, ev, reads, writes):
        for b in reads:
            if b.r.get(ev[0], 0) < ev[1]:
                b.r[ev[0]] = ev[1]
        for b in writes:
            b.w = ev
            b.r = {}

    def op(self, engname, fn, reads=(), writes=()):
        import os
        if getattr(self, "tag", None) is not None:
            st = os.environ.get("SKIPTAG")
            if st:
                tg, n = st.split(":")
                if tg == self.tag:
                    self.tagn = getattr(self, "tagn", 0) + 1
                    if self.tagn > int(n):
                        return None
        eng = self.E[engname]
        reads = _bufs(reads)
        writes = _bufs(writes)
        self._emit_waits(eng, self._collect(eng, reads, writes))
        eng.count += 1
        ev = (eng.sem, eng.count)
        s = self.sems[eng.sem]
        eng.prog.append(lambda h=eng.h, fn=fn, s=s: fn(h).then_inc(s, 1))
        self._mark(ev, reads, writes)
        self.ninstr += 1
        return ev

    def dma(self, qname, out, in_, extra_reads=(), extra_writes=(), **kw):
        eng = self.E[qname]
        reads = _bufs([in_] + list(extra_reads))
        writes = _bufs([out] + list(extra_writes))
        i = eng.dma_i % len(eng.dma_ring)
        eng.dma_i += 1
        semk = eng.dma_ring[i]
        waits = self._collect(eng, reads, writes)
        prev = eng.dma_last[i]
        if prev is not None and eng.known.get(prev[0], 0) < prev[1]:
            eng.known[prev[0]] = prev[1]
            waits.append(prev)
        self._emit_waits(eng, waits)
        self.semvals[semk] += 16
        ev = (semk, self.semvals[semk])
        eng.dma_last[i] = ev
        s = self.sems[semk]
        o, a = _ap(out), _ap(in_)
        eng.prog.append(lambda h=eng.h, o=o, a=a, s=s, kw=kw: h.dma_start(out=o, in_=a, **kw).then_inc(s, 16))
        self._mark(ev, reads, writes)
        self.ninstr += 1
        return ev

    def wait_all(self, engname, bufs):
        eng = self.E[engname]
        self._emit_waits(eng, self._collect(eng, _bufs(bufs), ()))

    def emit(self):
        nc = self.nc
        with nc.Block() as block:
            @block.tensor
            def _(e):
                for f in self.E["pe"].prog:
                    f()

            @block.vector
            def _(e):
                for f in self.E["dve"].prog:
                    f()

            @block.scalar
            def _(e):
                for f in self.E["act"].prog:
                    f()

            @block.gpsimd
            def _(e):
                for f in self.E["pool"].prog:
                    f()

            @block.sync
            def _(e):
                for f in self.E["sp"].prog:
                    f()


M0, R0, H0, S0 = 0, 1032, 2056, 3080


def _blk_k1024(Wc):
    return Wc.reshape(8, 128, 512).transpose(1, 0, 2).reshape(128, 4096)


class VecLayout:
    def __init__(self):
        self.off = {}
        self.n = 0

    def add(self, name, ncols):
        self.off[name] = self.n
        self.n += ncols


def vec_layout():
    v = VecLayout()
    v.add("norms", 32)
    v.add("bgate", 32)
    v.add("mconvw", 16)
    v.add("mconvb", 4)
    v.add("mig", 2)
    v.add("mfg", 2)
    v.add("mnorm", 2)
    v.add("rmu", 8)
    for n in ("rw0", "ra0", "rkk", "rka", "rrk", "rgnw", "rgnb", "rv0", "hnorm", "sd", "sbglu"):
        v.add(n, 2)
    v.add("fconvw", 132)
    v.add("fconvb", 44)
    return v


MAT_W2A2, MAT_G2, MAT_V1, MAT_V2, MAT_GLU, MAT_N = 0, 256, 512, 576, 832, 1344

C_ID, C_ONES, C_BO, C_MSU, C_MU, C_MSL, C_M2, C_RST, C_SGN, C_SWAP, C_TAU, C_N = (
    0, 128, 256, 384, 512, 640, 768, 896, 1408, 1409, 1537, 2049 + 8)
C_EPSR, C_EPSG, C_ONE, C_HPI, C_ZERO, C_TINY = 2049, 2050, 2051, 2052, 2053, 2054


def build_consts():
    c = np.zeros((128, C_N), np.float32)
    p = np.arange(128)
    c[:, C_ID:C_ID + 128] = np.eye(128)
    c[:, C_ONES:C_ONES + 128] = 1.0
    c[:, C_BO:C_BO + 128] = (p[:, None] // 64 == p[None, :] // 64)
    same = np.ones((128, 128), bool)
    c[:, C_MSU:C_MSU + 128] = same & (p[:, None] < p[None, :])
    c[:, C_MU:C_MU + 128] = same & (p[:, None] <= p[None, :])
    c[:, C_MSL:C_MSL + 128] = same & (p[None, :] < p[:, None])
    c[:, C_M2:C_M2 + 128] = ((p[:, None] % 64) <= (p[None, :] % 64))
    rst = np.ones(512, np.float32)
    rst[::128] = 0.0
    c[:, C_RST:C_RST + 512] = rst[None, :]
    c[:, C_SGN] = np.where(p < 64, 1.0, -1.0)
    c[:, C_SWAP:C_SWAP + 128] = (p[None, :] == (p[:, None] + 64) % 128)
    c[:, C_TAU:C_TAU + 512] = np.arange(1, 513, dtype=np.float32)[None, :]
    c[:, C_EPSR] = RMS_EPS
    c[:, C_EPSG] = GN_EPS
    c[:, C_ONE] = 1.0
    c[:, C_HPI] = math.pi / 2
    c[:, C_ZERO] = 0.0
    c[:, C_TINY] = 1e-12
    return c


def colvec(a, n):
    return np.ascontiguousarray(a.reshape(n, 128).T)


def ffn_chunk_order():
    order = []
    for i in range(11):
        order += [2 * i, 2 * i + 1, 22 + 2 * i, 22 + 2 * i + 1]
    return order


def host_pack(inp, L):
    f = lambda k: np.asarray(inp[k], np.float32)
    w_in, w_gate, w_branch, w_out, f_up, f_down = f("w_in"), f("w_gate"), f("w_branch"), f("w_out"), f("f_up"), f("f_down")
    wpack = np.zeros((L, NBLK, 128, 4096), np.float32)
    for l in range(L):
        Wi = w_in[l]
        blks = []
        blks.append(Wi[:, M0:M0 + 512])
        blks.append(np.concatenate([Wi[:, M0 + 768:M0 + 1024], Wi[:, H0 + 768:H0 + 1024]], 1))
        ig = np.concatenate([np.repeat(Wi[:, M0 + 1024 + h:M0 + 1025 + h], 64, 1) for h in range(4)], 1)
        fg = np.concatenate([np.repeat(Wi[:, M0 + 1028 + h:M0 + 1029 + h], 64, 1) for h in range(4)], 1)
        blks.append(np.concatenate([ig, fg], 1))
        blks.append(Wi[:, R0:R0 + 512])
        blks.append(Wi[:, R0 + 512:R0 + 1024])
        blks.append(Wi[:, H0:H0 + 512])
        blks.append(np.concatenate([Wi[:, S0:S0 + 256], np.zeros((1024, 256), np.float32)], 1))
        blks.append(np.concatenate([Wi[:, M0 + 512:M0 + 768], Wi[:, H0 + 512:H0 + 768]], 1))
        for i, b in enumerate(blks):
            wpack[l, i] = _blk_k1024(b)
        for b in range(4):
            for c in range(2):
                wpack[l, 8 + 2 * b + c] = _blk_k1024(w_gate[l, b][:, c * 512:(c + 1) * 512])
        for b in range(4):
            blk = wpack[l, 16 + b // 2].reshape(128, 8, 512)
            for kk in range(2):
                for m in range(8):
                    s = (b % 2) * 4 + kk * 2 + m // 4
                    j = m % 4
                    blk[:, s, j * 128:(j + 1) * 128] = w_branch[l, b][kk * 128:(kk + 1) * 128, m * 128:(m + 1) * 128]
        for c in range(2):
            wpack[l, 18 + c] = _blk_k1024(w_out[l][:, c * 512:(c + 1) * 512])
        for i in range(11):
            cols = np.concatenate([f_up[l][:, 256 * i:256 * i + 256], f_up[l][:, 2816 + 256 * i:2816 + 256 * i + 256]], 1)
            wpack[l, 20 + i] = _blk_k1024(cols)
        for c in range(2):
            for g in range(3):
                blk = wpack[l, 31 + c * 3 + g].reshape(128, 8, 512)
                for s in range(8):
                    k = 8 * g + s
                    if k < 22:
                        blk[:, s, :] = f_down[l][k * 128:(k + 1) * 128, c * 512:(c + 1) * 512]
    VL = vec_layout()
    vecs = np.zeros((128, L, VL.n), np.float32)
    order = ffn_chunk_order()
    for l in range(L):
        def put(name, arr):
            vecs[:, l, VL.off[name]:VL.off[name] + arr.shape[1]] = arr
        put("norms", np.concatenate([colvec(f("norms")[l, j], 8) for j in range(4)], 1))
        put("bgate", np.concatenate([colvec(f("b_gate")[l, b], 8) for b in range(4)], 1))
        cw = f("m_conv_w")[l]
        put("mconvw", np.stack([colvec(cw[j], 4) for j in range(4)], 2).reshape(128, 16))
        put("mconvb", colvec(f("m_conv_b")[l], 4))
        gb = f("m_gate_b")[l]
        put("mig", colvec(np.repeat(gb[0], 64), 2))
        put("mfg", colvec(np.repeat(gb[1], 64), 2))
        put("mnorm", colvec(f("m_norm")[l], 2))
        put("rmu", colvec(f("r_mu")[l], 8))
        for n, k in (("rw0", "r_w0"), ("ra0", "r_a0"), ("rkk", "r_kk"), ("rka", "r_ka"), ("rrk", "r_rk"),
                     ("rgnw", "r_gn_w"), ("rgnb", "r_gn_b"), ("hnorm", "h_norm"), ("sd", "s_d"), ("sbglu", "s_b_glu")):
            put(n, colvec(f(k)[l], 2))
        if l >= 1:
            put("rv0", colvec(f("r_v0")[l - 1], 2))
        fcw = f("f_conv_w")[l]
        fcb = f("f_conv_b")[l]
        put("fconvw", np.stack([colvec(fcw[j], 44)[:, order] for j in range(3)], 2).reshape(128, 132))
        put("fconvb", colvec(fcb, 44)[:, order])
    hlb = np.ascontiguousarray(np.stack([colvec(f("h_lb")[l], 2) for l in range(L)], 2))
    mats = np.zeros((L, 128, MAT_N), np.float32)
    for l in range(L):
        mats[l, 0:64, MAT_W2A2:MAT_W2A2 + 256] = f("r_w2")[l]
        mats[l, 64:128, MAT_W2A2:MAT_W2A2 + 256] = f("r_a2")[l]
        mats[l, :, MAT_G2:MAT_G2 + 256] = f("r_g2")[l]
        if l >= 1:
            mats[l, :, MAT_V1:MAT_V1 + 64] = f("r_v1")[l - 1].reshape(2, 128, 32).transpose(1, 0, 2).reshape(128, 64)
            mats[l, 0:32, MAT_V2:MAT_V2 + 256] = f("r_v2")[l - 1]
        mats[l, :, MAT_GLU:MAT_GLU + 512] = f("s_w_glu")[l].reshape(2, 128, 256).transpose(1, 0, 2).reshape(128, 512)
    s5row = np.zeros((L, 3, 1024), np.float32)
    s5col = np.zeros((L, 128, 3 * G), np.float32)
    s5b = np.zeros((L, 128, 2, G, 64), np.float32)
    s5c = np.zeros((L, G, 128, 256), np.float32)
    for l in range(L):
        are, aim, ls = f("s_a_re")[l], f("s_a_im")[l], f("s_log_step")[l]
        s5row[l, 0] = are.reshape(-1)
        s5row[l, 1] = aim.reshape(-1)
        s5row[l, 2] = np.repeat(ls, 64)
        s5col[l, :, 0:G] = np.concatenate([are.T, are.T], 0)
        s5col[l, :, G:2 * G] = np.concatenate([aim.T, aim.T], 0)
        s5col[l, :, 2 * G:3 * G] = np.broadcast_to(ls[None, :], (128, G))
        bre, bim = f("s_b_re")[l], f("s_b_im")[l]
        cre, cim = f("s_c_re")[l], f("s_c_im")[l]
        for g in range(G):
            r0 = 16 * (g % 8)
            s5b[l, r0:r0 + 16, 0, g, :] = bre[g].T
            s5b[l, r0:r0 + 16, 1, g, :] = bim[g].T
            s5c[l, g, 0:64, r0:r0 + 16] = cre[g].T
            s5c[l, g, 64:128, r0:r0 + 16] = cim[g].T
            s5c[l, g, 0:64, 128 + r0:128 + r0 + 16] = cim[g].T
            s5c[l, g, 64:128, 128 + r0:128 + r0 + 16] = cre[g].T
    return {"wpack": wpack.reshape(L * NBLK, 128, 4096), "vecs": vecs.reshape(128, L * VL.n), "hlb": hlb.reshape(128, 2 * L),
            "mats": mats, "s5row": s5row, "s5col": s5col, "s5b": s5b.reshape(L, 128, 2 * G * 64), "s5c": s5c,
            "consts": build_consts()}


def host_x(x_b, meta, NT):
    TP = NT * LT
    xT = np.zeros((D, TP), np.float32)
    xT[:, PADF:PADF + NMETA] = np.asarray(meta, np.float32).T
    xT[:, LT:] = np.asarray(x_b, np.float32).T
    return xT


class Prog:
    def __init__(self, NT, L, mixers="mrhs", debug=False):
        self.NT, self.L, self.mixers = NT, L, mixers
        self.TP = NT * LT
        nc = bass.Bass("TRN2", target_bir_lowering=False)
        self.nc = nc
        fw = FW(nc)
        self.fw = fw
        VL = vec_layout()
        self.VL = VL
        TP = self.TP
        self.xT = fw.dram("xT", [D, TP], kind="ExternalInput")
        self.wpack = fw.dram("wpack", [L * NBLK, 128, 4096], kind="ExternalInput")
        self.vecs_d = fw.dram("vecs", [128, L * VL.n], kind="ExternalInput")
        self.hlb_d = fw.dram("hlb", [128, 2 * L], kind="ExternalInput")
        self.mats_d = fw.dram("mats", [L, 128, MAT_N], kind="ExternalInput")
        self.s5row_d = fw.dram("s5row", [L, 3, 1024], kind="ExternalInput")
        self.s5col_d = fw.dram("s5col", [L, 128, 3 * G], kind="ExternalInput")
        self.s5b_d = fw.dram("s5b", [L, 128, 2 * G * 64], kind="ExternalInput")
        self.s5c_d = fw.dram("s5c", [L, G, 128, 256], kind="ExternalInput")
        self.consts_d = fw.dram("consts", [128, C_N], kind="ExternalInput")
        self.out_d = fw.dram("outT", [D, TP], kind="ExternalOutput")
        self.wbf = [fw.dram("wbf%d" % l, [NBLK, 128, 4096], BF16) for l in range(L)]
        self.hA = fw.dram("hA", [D, TP])
        self.hB = fw.dram("hB", [D, TP])
        self.vfirst = fw.dram("vfirst", [256, TP])
        S5W = 256 + 256 + 1024 + 128
        self.S5W = S5W
        self.s5s = fw.dram("s5s", [G, 128, S5W])
        self.s5sb = fw.dram("s5sb", [G, 128, 512], BF16)
        self.consts = fw.sbuf("consts_s", [128, C_N])
        self.vecs = fw.sbuf("vecs_s", [128, L * VL.n])
        self.mats = fw.sbuf("mats_s", [128, MAT_N])
        self.lbs = fw.sbuf("lbs_s", [128, 6 * L])
        self.hT = fw.sbuf("hT", [128, 8, LT])
        self.ub = fw.sbuf("ub", [128, 8, LT], BF16)
        self.wring = [fw.sbuf("wr%d" % i, [128, 4096], BF16) for i in range(NWR)]
        self.wi = 0
        self.ybf = [fw.sbuf("ybf%d" % i, [128, 2, LT], BF16) for i in range(4)]
        self.mergedb = fw.sbuf("mergedb", [128, 8, LT], BF16)
        self.qkpre = fw.sbuf("qkpre", [128, 4, 516])
        self.vaug = fw.sbuf("vaug", [128, 4, 512])
        self.vtokh = fw.sbuf("vtokh", [128, 4, 256])
        self.ar = [fw.sbuf("ar%d" % i, [128, 2, LT]) for i in range(2)]
        self.art = [fw.sbuf("art%d" % i, [128, 2, LT]) for i in range(2)]
        self.s5ring = [fw.sbuf("s5r%d" % i, [128, S5W]) for i in range(2)]
        self.s5i = 0
        self.s5ringb = [fw.sbuf("s5rb%d" % i, [128, 512], BF16) for i in range(2)]
        self.sqb = [fw.sbuf("sqb%d" % i, [128, LT], BF16) for i in range(2)]
        self.cvb = [fw.sbuf("cvb%d" % i, [128, LT], BF16) for i in range(2)]
        self.bg = None
        self.sqi = 0
        self.onesb = fw.sbuf("onesb", [128, 128], BF16)
        self.bob = fw.sbuf("bob", [128, 128], BF16)
        self.stm = [fw.sbuf("stm%d" % i, [128, 128]) for i in range(4)]
        self.sth = [fw.sbuf("sth%d" % i, [128, 64]) for i in range(4)]
        self.strw = [fw.sbuf("str%d" % i, [128, 64]) for i in range(4)]
        self.kz = [fw.sbuf("kz%d" % i, [128, LT]) for i in range(2)]
        self.bz = [fw.sbuf("bz%d" % i, [128, LT]) for i in range(2)]
        self.s5carry = fw.sbuf("s5carry", [128, G])
        self.s5rho = fw.sbuf("s5rho", [128, 2 * G])
        self.rhalo = fw.sbuf("rhalo", [128, 8])
        self.fhalo = fw.sbuf("fhalo", [128, 44, 2])
        self.small = fw.sbuf("small", [128, 64])
        self.prt = [fw.sbuf("prt%d" % i, [128, LT], F32R) for i in range(4)]
        rem = nc.sbuf_bytes_remaining
        NW = (rem - 1024) // 2048
        self.NW = NW
        self.pool_t = [fw.sbuf("w%d" % i, [128, LT]) for i in range(NW)]
        self.free_w = list(self.pool_t)
        self.ps_t = [fw.psum("ps%d" % i, [128, LT]) for i in range(8)]
        self.free_ps = list(self.ps_t)
        self.dbg = {}
        self.debug = debug

    def wt(self):
        assert self.free_w, "work pool exhausted"
        return self.free_w.pop(0)

    def fr(self, *ts):
        for t in ts:
            assert t not in self.free_w
            self.free_w.append(t)

    def ps(self):
        assert self.free_ps, "psum pool exhausted"
        return self.free_ps.pop(0)

    def pf(self, *ts):
        for t in ts:
            self.free_ps.append(t)

    def cc(self, col, n=1, rows=slice(0, 128)):
        return self.consts[rows, col:col + n]

    def vv(self, l, name, j=0, rows=slice(0, 128)):
        c = l * self.VL.n + self.VL.off[name] + j
        return self.vecs[rows, c:c + 1]

    def mm(self, out, lhsT, rhs, start=True, stop=True):
        self.fw.op("pe", lambda h: h.matmul(out.ap, lhsT.ap, rhs.ap, start=start, stop=stop),
                   reads=[lhsT, rhs], writes=[out])

    def transpose(self, out, in_):
        n = in_.ap.shape[0]
        ident = self.consts[0:n, C_ID:C_ID + n]
        self.fw.op("pe", lambda h: h.transpose(out.ap, in_.ap, ident.ap), reads=[in_, ident], writes=[out])

    def act(self, out, in_, func, bias=None, scale=1.0, eng="act"):
        kw = {}
        reads = [in_]
        if bias is not None:
            if isinstance(bias, View):
                kw["bias"] = bias.ap
                reads.append(bias)
            else:
                kw["bias"] = bias
        if isinstance(scale, View):
            kw["scale"] = scale.ap
            reads.append(scale)
        elif scale != 1.0:
            kw["scale"] = scale
        self.fw.op("act", lambda h: h.activation(out.ap, in_.ap, func, **kw), reads=reads, writes=[out])

    def tt(self, out, in0, in1, op, eng="dve"):
        self.fw.op(eng, lambda h: h.tensor_tensor(out.ap, in0.ap, in1.ap, op), reads=[in0, in1], writes=[out])

    def ts(self, out, in0, s1, s2=None, op0=ALU.mult, op1=None, eng="dve"):
        reads = [in0]
        a1 = s1
        if isinstance(s1, View):
            a1 = s1.ap
            reads.append(s1)
        a2 = s2
        if isinstance(s2, View):
            a2 = s2.ap
            reads.append(s2)
        if op1 is None:
            self.fw.op(eng, lambda h: h.tensor_scalar(out.ap, in0.ap, a1, None, op0), reads=reads, writes=[out])
        else:
            self.fw.op(eng, lambda h: h.tensor_scalar(out.ap, in0.ap, a1, a2, op0, op1), reads=reads, writes=[out])

    def stt(self, out, in0, scalar, in1, op0, op1):
        reads = [in0, in1]
        a = scalar
        if isinstance(scalar, View):
            a = scalar.ap
            reads.append(scalar)
        self.fw.op("dve", lambda h: h.scalar_tensor_tensor(out.ap, in0.ap, a, in1.ap, op0, op1), reads=reads, writes=[out])

    def copy(self, out, in_, eng="act"):
        if eng == "act":
            self.fw.op("act", lambda h: h.copy(out.ap, in_.ap), reads=[in_], writes=[out])
        else:
            self.fw.op(eng, lambda h: h.tensor_copy(out.ap, in_.ap), reads=[in_], writes=[out])

    def scan(self, out, d0, d1, init, op0=ALU.mult, op1=ALU.add):
        reads = [d0, d1]
        a = init
        if isinstance(init, View):
            a = init.ap
            reads.append(init)
        self.fw.op("dve", lambda h: h.tensor_tensor_scan(out.ap, d0.ap, d1.ap, a, op0, op1), reads=reads, writes=[out])

    def recip(self, out, in_):
        self.fw.op("dve", lambda h: h.reciprocal(out.ap, in_.ap), reads=[in_], writes=[out])

    def memset(self, out, val, eng="pool"):
        self.fw.op(eng, lambda h: h.memset(out.ap, val), writes=[out])

    def load(self, out, in_, q="sp", **kw):
        self.fw.dma(q, out, in_, **kw)

    def dump(self, name, view, shape):
        if not self.debug:
            return
        d = self.fw.dram("dbg_" + name, list(shape), kind="ExternalOutput")
        self.dbg[name] = d
        self.fw.dma("pool", d[:], view)

    def wload(self, l, bi):
        slot = self.wring[self.wi % len(self.wring)]
        self.wi += 1
        self.load(slot[:], self.wbf[l][bi])
        return slot

    def dense_fm(self, slot, rhs, ms, consume):
        for m in ms:
            ps = self.ps()
            for k in range(8):
                self.mm(ps[:, :], slot[:, k * 512 + m * 128:k * 512 + (m + 1) * 128], rhs[:, k, :],
                        start=(k == 0), stop=(k == 7))
            consume(m, ps)
            self.pf(ps)

    def sq_tile(self):
        t = self.sqb[self.sqi % 2]
        self.sqi += 1
        return t

    def rstd_from(self, ps_ss, inv_n, eps_col):
        r = self.wt()
        self.act(r[:, :], ps_ss[:, :], AF.Sqrt, bias=self.cc(eps_col), scale=inv_n)
        self.recip(r[:, :], r[:, :])
        return r

    def rmsnorm(self, src, l, nidx, dst):
        ps = self.ps()
        for k in range(8):
            sq = self.sq_tile()
            self.act(sq[:, :], src(k), AF.Square)
            self.mm(ps[:, :], self.onesb[:, :], sq[:, :], start=(k == 0), stop=(k == 7))
        r = self.rstd_from(ps, 1.0 / D, C_EPSR)
        self.pf(ps)
        for k in range(8):
            self.stt(dst(k), src(k), self.vv(l, "norms", nidx * 8 + k), r[:, :], ALU.mult, ALU.mult)
        self.fr(r)

    def headnorm_rstd(self, hm, eps_col):
        sq = self.sq_tile()
        self.act(sq[:, :], hm[:, :], AF.Square)
        ps = self.ps()
        self.mm(ps[:, :], self.bob[:, :], sq[:, :])
        r = self.rstd_from(ps, 1.0 / 64, eps_col)
        self.pf(ps)
        return r

    def prologue(self):
        L = self.L
        self.load(self.consts[:, :], self.consts_d[:, :])
        self.load(self.vecs[:, :], self.vecs_d[:, :])
        self.copy(self.onesb[:, :], self.cc(C_ONES, 128), eng="dve")
        self.copy(self.bob[:, :], self.cc(C_BO, 128), eng="dve")
        hl = self.small
        self.load(hl[:, 0:2 * L], self.hlb_d[:, :])
        e = self.wt()
        self.act(e[:, 0:2 * L], hl[:, 0:2 * L], AF.Exp)
        for c in range(2):
            ssum = e[:, 32 + c:33 + c]
            self.fw.op("dve", lambda h, c=c, ssum=ssum: h.tensor_reduce(ssum.ap, e[:, c * L:(c + 1) * L].ap, mybir.AxisListType.X, ALU.add),
                       reads=[e], writes=[e])
            self.recip(ssum, ssum)
            self.ts(e[:, c * L:(c + 1) * L], e[:, c * L:(c + 1) * L], ssum, None, ALU.mult)
            self.memset(self.lbs[:, c * L:c * L + 1], 0.0, eng="dve")
            for l in range(1, L):
                self.tt(self.lbs[:, c * L + l:c * L + l + 1], self.lbs[:, c * L + l - 1:c * L + l], e[:, c * L + l:c * L + l + 1], ALU.add)
        self.ts(self.lbs[:, 2 * L:4 * L], self.lbs[:, 0:2 * L], -1.0, 1.0, ALU.mult, ALU.add)
        self.ts(self.lbs[:, 4 * L:6 * L], self.lbs[:, 0:2 * L], 1.0, -1.0, ALU.mult, ALU.add)
        self.fr(e)
        nst = min(3, self.NW // 8)
        stg = [self.pool_t[8 * i:8 * i + 8] for i in range(nst)]
        engs = ["act", "dve", "pool"]
        for b in range(NBLK):
            st = stg[b % nst]
            for j in range(8):
                self.load(st[j][:, :], self.wpack[b][:, j * 512:(j + 1) * 512])
            slot = self.wring[b % 2]
            for j in range(8):
                self.copy(slot[:, j * 512:(j + 1) * 512], st[j][:, :], eng=engs[(b * 8 + j) % 3])
            self.fw.dma("pool", self.wbf[0][b], slot[:, :])
        self.memset(self.vaug[:, :, :], 1.0, eng="dve")
        for t in self.kz + self.bz:
            self.memset(t[:, :], 0.0, eng="dve")

    def layer_prep(self, l):
        self.load(self.mats[:, :], self.mats_d[l])
        for t in self.stm + self.sth + self.strw:
            self.memset(t[:, :], 0.0, eng="dve")
        self.memset(self.s5carry[:, :], 0.0, eng="dve")
        self.memset(self.rhalo[:, :], 0.0, eng="dve")
        self.memset(self.fhalo[:, :, :], 0.0, eng="dve")
        self.memset(self.qkpre[:, :, 0:3], 0.0, eng="dve")
        if "s" in self.mixers:
            self.s5_prep(l)

    def range_reduce(self, x, tmp, tmpi):
        C1 = 6.28125
        C2 = TWO_PI - C1
        self.ts(tmp, x, 1.0 / TWO_PI, 0.5, ALU.mult, ALU.add)
        self.copy(tmpi, tmp, eng="dve")
        self.copy(tmp, tmpi, eng="dve")
        self.stt(x, tmp, -C1, x, ALU.mult, ALU.add)
        self.stt(x, tmp, -C2, x, ALU.mult, ALU.add)
        self.ts(tmp, x, math.pi, -TWO_PI, ALU.is_gt, ALU.mult)
        self.tt(x, x, tmp, ALU.add)
        self.ts(tmp, x, -math.pi, TWO_PI, ALU.is_lt, ALU.mult)
        self.tt(x, x, tmp, ALU.add)
        self.ts(x, x, math.pi, -math.pi, ALU.min, ALU.max)

    def s5_prep(self, l):
        are, aim, stp, t1, t2, t3, t4, t5 = [self.wt() for _ in range(8)]
        A = lambda t: t[:, :].bc([128, 1024]) if False else t
        cre = [self.wt(), self.wt()]
        cim = [self.wt(), self.wt()]
        ti = self.wt()
        for hf in range(2):
            sl = slice(hf * 512, (hf + 1) * 512)
            self.load(are[:, :], self.s5row_d[l, 0:1, sl].bc([128, 512]))
            self.load(aim[:, :], self.s5row_d[l, 1:2, sl].bc([128, 512]))
            self.load(stp[:, :], self.s5row_d[l, 2:3, sl].bc([128, 512]))
            self.act(stp[:, :], stp[:, :], AF.Exp)
            self.tt(t1[:, :], are[:, :], stp[:, :], ALU.mult)
            self.act(t1[:, :], t1[:, :], AF.Exp)
            self.tt(t2[:, :], aim[:, :], stp[:, :], ALU.mult)
            self.ts(t3[:, :], t2[:, :], math.pi / 2, None, ALU.add)
            self.range_reduce(t2[:, :], t4[:, :], ti[:, :].m(lambda a: a.bitcast(I32)))
            self.range_reduce(t3[:, :], t4[:, :], ti[:, :].m(lambda a: a.bitcast(I32)))
            self.act(t2[:, :], t2[:, :], AF.Sin)
            self.act(t3[:, :], t3[:, :], AF.Sin)
            self.tt(t2[:, :], t2[:, :], t1[:, :], ALU.mult)
            self.tt(t3[:, :], t3[:, :], t1[:, :], ALU.mult)
            self.ts(t3[:, :], t3[:, :], -1.0, None, ALU.add)
            self.tt(t4[:, :], are[:, :], are[:, :], ALU.mult)
            self.tt(t5[:, :], aim[:, :], aim[:, :], ALU.mult)
            self.tt(t4[:, :], t4[:, :], t5[:, :], ALU.add)
            self.recip(t4[:, :], t4[:, :])
            self.tt(t5[:, :], t3[:, :], are[:, :], ALU.mult)
            self.tt(t1[:, :], t2[:, :], aim[:, :], ALU.mult)
            self.tt(t5[:, :], t5[:, :], t1[:, :], ALU.add)
            self.tt(cre[hf][:, :], t5[:, :], t4[:, :], ALU.mult)
            self.tt(t5[:, :], t2[:, :], are[:, :], ALU.mult)
            self.tt(t1[:, :], t3[:, :], aim[:, :], ALU.mult)
            self.tt(t5[:, :], t5[:, :], t1[:, :], ALU.subtract)
            self.tt(cim[hf][:, :], t5[:, :], t4[:, :], ALU.mult)
        bre, bim = are, aim
        for hf in range(2):
            self.load(bre[:, :], self.s5b_d[l][:, hf * 512:(hf + 1) * 512])
            self.load(bim[:, :], self.s5b_d[l][:, 1024 + hf * 512:1024 + (hf + 1) * 512])
            self.tt(t1[:, :], cre[hf][:, :], bre[:, :], ALU.mult)
            self.tt(t2[:, :], cim[hf][:, :], bim[:, :], ALU.mult)
            self.tt(t1[:, :], t1[:, :], t2[:, :], ALU.subtract)
            self.tt(t2[:, :], cre[hf][:, :], bim[:, :], ALU.mult)
            self.tt(t3[:, :], cim[hf][:, :], bre[:, :], ALU.mult)
            self.tt(t2[:, :], t2[:, :], t3[:, :], ALU.add)
            self.ts(t3[:, :], t1[:, :], -1.0, None, ALU.mult)
            for gg in range(8):
                g = hf * 8 + gg
                sl = slice(gg * 64, (gg + 1) * 64)
                self.fw.dma("pool", self.s5s[g][:, 0:64], t1[:, sl])
                self.fw.dma("pool", self.s5s[g][:, 64:128], t2[:, sl])
                self.fw.dma("pool", self.s5s[g][:, 128:192], t2[:, sl])
                self.fw.dma("pool", self.s5s[g][:, 192:256], t3[:, sl])
        for g in range(G):
            self.fw.dma("pool", self.s5s[g][:, 256:512], self.s5c_d[l, g])
        for g in range(G):
            self.load(t1[:, :], self.s5s[g][:, 0:512])
            tb = t2[:, :].m(lambda ap: ap.bitcast(BF16))[:, 0:512]
            self.copy(tb, t1[:, :], eng="pool")
            self.fw.dma("pool", self.s5sb[g], tb)
        sc = self.small
        self.load(sc[:, 0:3 * G], self.s5col_d[l])
        self.act(sc[:, 2 * G:3 * G], sc[:, 2 * G:3 * G], AF.Exp)
        self.tt(self.s5rho[:, 0:G], sc[:, G:2 * G], sc[:, 2 * G:3 * G], ALU.mult)
        self.tt(self.s5rho[:, G:2 * G], sc[:, 0:G], sc[:, 2 * G:3 * G], ALU.mult)
        self.act(self.s5rho[:, G:2 * G], self.s5rho[:, G:2 * G], AF.Exp)
        tau = self.cc(C_TAU, 512)
        for g in range(G):
            th = self.s5rho[:, g:g + 1]
            self.ts(t1[:, :], tau, th, None, ALU.mult)
            self.ts(t2[:, :], t1[:, :], math.pi / 2, None, ALU.add)
            self.range_reduce(t1[:, :], t4[:, :], ti[:, :].m(lambda a: a.bitcast(I32)))
            self.range_reduce(t2[:, :], t4[:, :], ti[:, :].m(lambda a: a.bitcast(I32)))
            self.act(t1[:, :], t1[:, :], AF.Sin)
            self.act(t2[:, :], t2[:, :], AF.Sin)
            self.ts(t3[:, 0:128], self.cc(C_ID, 128), t2[:, 511:512], None, ALU.mult)
            self.ts(t3[:, 128:129], t1[:, 511:512], self.cc(C_SGN), None, ALU.mult)
            self.stt(t3[:, 0:128], self.cc(C_SWAP, 128), t3[:, 128:129], t3[:, 0:128], ALU.mult, ALU.add)
            self.fw.dma("pool", self.s5s[g][:, 512:1024], t2[:, :])
            self.fw.dma("pool", self.s5s[g][:, 1024:1536], t1[:, :])
            self.fw.dma("pool", self.s5s[g][:, 1536:1664], t3[:, 0:128])
        self.fr(are, aim, stp, t1, t2, t3, t4, t5, ti, *cre, *cim)

    def cla(self, Qt, Qh, Kt, kz, vnum, vden, vst, STz, nst, dec, ps_num, ps_den, clamp=False):
        for blk in self.blks:
            tok = slice(blk * 128, (blk + 1) * 128)
            for hh in range(2):
                hp = slice(64 * hh, 64 * hh + 64)
                pp = self.ps()
                self.mm(pp[:, 0:128], Kt[hp, tok], Qt[hp, tok])
                pm = self.wt()
                if clamp:
                    self.ts(pm[:, 0:128], pp[:, 0:128], 1e30, -1e30, ALU.min, ALU.max)
                    self.tt(pm[:, 0:128], pm[:, 0:128], self.cc(C_MU, 128), ALU.mult)
                else:
                    self.tt(pm[:, 0:128], pp[:, 0:128], self.cc(C_MU, 128), ALU.mult)
                self.pf(pp)
                import os
                CS = int(os.environ.get("CSTG", "9"))
                if CS >= 2:
                    self.mm(ps_num[hp, tok], vnum(blk, hh), pm[:, 0:128], start=True, stop=(CS < 3))
                else:
                    self.mm(ps_num[hp, tok], self.cc(C_BO + 64 * hh, 64), pm[:, 0:128], start=True, stop=True)
                if CS >= 3:
                    self.mm(ps_num[hp, tok], STz[hh][:, 0:64], Qh[:, tok], start=False, stop=True)
                if vden is not None:
                    self.mm(ps_den[hp, tok], vden(blk, hh), pm[:, 0:128], start=True, stop=False)
                    self.mm(ps_den[hp, tok], STz[hh][:, 64:128], Qh[:, tok], start=False, stop=True)
                if CS >= 4:
                    pst = self.ps()
                    self.mm(pst[:, 0:nst], kz[hh][:, tok], vst(blk, hh))
                    self.stt(STz[hh][:, 0:nst], STz[hh][:, 0:nst], dec[:, blk:blk + 1], pst[:, 0:nst], ALU.mult, ALU.add)
                    self.pf(pst)
                self.fr(pm)
                yield "CHAIN"

    def to_tok(self, src, dst):
        for blk in self.blks:
            pt = self.ps()
            self.transpose(pt[:, 0:128], src[:, blk * 128:(blk + 1) * 128])
            self.copy(dst[:, blk * 128:(blk + 1) * 128], pt[:, 0:128])
            self.pf(pt)

    def to_tok_z(self, src, dz):
        for blk in self.blks:
            pt = self.ps()
            self.transpose(pt[:, 0:128], src[:, blk * 128:(blk + 1) * 128])
            self.copy(dz[0][:, blk * 128:blk * 128 + 64], pt[:, 0:64])
            self.copy(dz[1][:, blk * 128 + 64:blk * 128 + 128], pt[:, 64:128])
            self.pf(pt)

    def mlstm(self, l, i):
        slot = self.wload(l, 0)
        self.dense_fm(slot, self.ub, range(4), lambda m, ps: self.copy(self.qkpre[:, m, 3:515], ps[:, :]))
        qk = []
        for m in range(4):
            acc = self.wt()
            self.ts(acc[:, :], self.qkpre[:, m, 0:512], self.vv(l, "mconvw", m * 4 + 0), self.vv(l, "mconvb", m), ALU.mult, ALU.add)
            for j in range(1, 4):
                self.stt(acc[:, :], self.qkpre[:, m, j:j + 512], self.vv(l, "mconvw", m * 4 + j), acc[:, :], ALU.mult, ALU.add)
            self.act(acc[:, :], acc[:, :], AF.Silu)
            qk.append(acc)
            self.copy(self.qkpre[:, m, 0:3], self.qkpre[:, m, 512:515], eng="pool")
        slot = self.wload(l, 2)
        li, lf = [None, None], [None, None]

        def gcons(m, ps):
            t = self.wt()
            if m < 2:
                self.act(t[:, :], ps[:, :], AF.Identity, bias=self.vv(l, "mig", m))
                li[m] = t
            else:
                self.act(t[:, :], ps[:, :], AF.Sigmoid, bias=self.vv(l, "mfg", m - 2))
                self.act(t[:, :], t[:, :], AF.Ln)
                lf[m - 2] = t
        self.dense_fm(slot, self.ub, range(4), gcons)
        slot = self.wload(l, 7)
        for blk in self.blks:
            ps = self.ps()
            for k in range(8):
                self.mm(ps[:, :], self.ub[:, k, blk * 128:(blk + 1) * 128], slot[:, k * 512:(k + 1) * 512], start=(k == 0), stop=(k == 7))
            self.copy(self.vaug[:, blk, :].re("p (h e) -> p h e", e=128)[:, :, 0:64], ps[:, 0:256].re("p (h e) -> p h e", e=64))
            self.copy(self.vtokh[:, blk, :], ps[:, 256:512])
            self.pf(ps)
        slot = self.wload(l, 1)
        og, hg = [None, None], [None, None]

        def ocons(m, ps):
            t = self.wt()
            if m < 2:
                self.act(t[:, :], ps[:, :], AF.Sigmoid)
                og[m] = t
            else:
                self.act(t[:, :], ps[:, :], AF.Silu)
                hg[m - 2] = t
        self.dense_fm(slot, self.ub, range(4), ocons)
        self.hg = hg
        if "m" not in self.mixers:
            self.fr(*qk, *li, *lf, *og)
            self.memset(self.ybf[0][:, :, :], 0.0)
            return
        for c in range(2):
            q, k = qk[c], qk[2 + c]
            b = lf[c]
            self.scan(b[:, :], self.cc(C_RST, 512), lf[c][:, :], 0.0)
            f1 = li[c]
            self.tt(f1[:, :], li[c][:, :], b[:, :], ALU.subtract)
            self.act(f1[:, :], f1[:, :], AF.Exp)
            eb = b
            self.act(eb[:, :], b[:, :], AF.Exp)
            self.stt(q[:, :], q[:, :], 0.125, eb[:, :], ALU.mult, ALU.mult)
            self.tt(k[:, :], k[:, :], f1[:, :], ALU.mult)
            if i == 0:
                self.memset(k[:, 0:PADF], 0.0, eng="dve")
            kh = f1
            eb_last = eb[:, :].re("p (c t) -> p c t", t=128)[:, :, 127:128]
            self.tt(kh[:, :].re("p (c t) -> p c t", t=128), k[:, :].re("p (c t) -> p c t", t=128), eb_last.bc([128, 4, 128]), ALU.mult)
            self.to_tok_z(kh, self.kz)
            self.fr(kh)
            ps_num, ps_den = self.ps(), self.ps()
            dec = eb[:, :].re("p (c t) -> p c t", t=128)[:, :, 127]
            vn = lambda blk, hh, c=c: self.vaug[:, blk, (2 * c + hh) * 128:(2 * c + hh) * 128 + 64]
            vd = lambda blk, hh, c=c: self.vaug[:, blk, (2 * c + hh) * 128 + 64:(2 * c + hh) * 128 + 128]
            vs = lambda blk, hh, c=c: self.vaug[:, blk, (2 * c + hh) * 128:(2 * c + hh) * 128 + 128]
            yield from self.cla(q, q, k, self.kz, vn, vd, vs, self.stm[2 * c:2 * c + 2], 128, dec, ps_num, ps_den)
            self.fr(eb, k)
            a = q
            self.act(a[:, :], ps_den[:, :], AF.Abs)
            self.ts(a[:, :], a[:, :], 1.0, None, ALU.max)
            self.recip(a[:, :], a[:, :])
            hm = self.wt()
            self.tt(hm[:, :], ps_num[:, :], a[:, :], ALU.mult)
            self.pf(ps_num, ps_den)
            r = self.headnorm_rstd(hm, C_EPSR)
            self.stt(hm[:, :], hm[:, :], self.vv(l, "mnorm", c), r[:, :], ALU.mult, ALU.mult)
            self.tt(self.ybf[0][:, c, :], hm[:, :], og[c][:, :], ALU.mult)
            self.fr(a, hm, r, og[c])

    def hgrn2(self, l, i):
        L = self.L
        slot = self.wload(l, 5)
        qs, kk_, lfs = [None, None], [None, None], [None, None]

        def cons(m, ps):
            if m < 2:
                t = self.wt()
                self.act(t[:, :], ps[:, :], AF.Silu)
                qs[m] = t
            else:
                c = m - 2
                sg, lf_, k = self.wt(), self.wt(), self.wt()
                self.act(sg[:, :], ps[:, :], AF.Sigmoid)
                self.ts(lf_[:, :], sg[:, :], self.lbs[:, 2 * L + c * L + l:2 * L + c * L + l + 1], self.lbs[:, c * L + l:c * L + l + 1], ALU.mult, ALU.add)
                self.act(lf_[:, :], lf_[:, :], AF.Ln)
                self.ts(k[:, :], sg[:, :], self.lbs[:, 4 * L + c * L + l:4 * L + c * L + l + 1], self.lbs[:, 2 * L + c * L + l:2 * L + c * L + l + 1], ALU.mult, ALU.add)
                self.fr(sg)
                kk_[c], lfs[c] = k, lf_
        self.dense_fm(slot, self.ub, range(4), cons)
        hg = self.hg
        yield "PREPDONE"
        self.fw.tag = None
        if "h" not in self.mixers:
            self.fr(*qs, *kk_, *lfs, *hg)
            self.memset(self.ybf[2][:, :, :], 0.0)
            return
        for c in range(2):
            g = lfs[c]
            self.scan(g[:, :], self.cc(C_RST, 512), lfs[c][:, :], 0.0)
            g3 = g[:, :].re("p (c t) -> p c t", t=128)
            r3 = lambda t: t[:, :].re("p (c t) -> p c t", t=128)
            ein, gd, qt, kt, kh = self.wt(), self.wt(), self.wt(), self.wt(), self.wt()
            self.act(ein[:, :], g[:, :], AF.Exp)
            self.tt(r3(gd), g3, g3[:, :, 63:64].bc([128, 4, 128]), ALU.subtract)
            self.act(qt[:, :], gd[:, :], AF.Exp)
            self.tt(qt[:, :], qt[:, :], qs[c][:, :], ALU.mult)
            self.act(kt[:, :], gd[:, :], AF.Exp, scale=-1.0)
            self.tt(kt[:, :], kt[:, :], kk_[c][:, :], ALU.mult)
            self.tt(r3(gd), g3[:, :, 127:128].bc([128, 4, 128]), g3, ALU.subtract)
            self.act(kh[:, :], gd[:, :], AF.Exp)
            self.tt(kh[:, :], kh[:, :], kk_[c][:, :], ALU.mult)
            qh = qs[c]
            self.tt(qh[:, :], qs[c][:, :], ein[:, :], ALU.mult)
            self.to_tok_z(kh, self.kz)
            self.fr(kh, kk_[c], g, gd)
            ps_num = self.ps()
            dec = ein[:, :].re("p (c t) -> p c t", t=128)[:, :, 127]
            vn = lambda blk, hh, c=c: self.vtokh[:, blk, (2 * c + hh) * 64:(2 * c + hh) * 64 + 64]
            yield from self.cla(qt, qh, kt, self.kz, vn, None, vn, self.sth[2 * c:2 * c + 2], 64, dec, ps_num, None, clamp=True)
            self.fr(ein, qt, kt)
            hm = qh
            self.copy(hm[:, :], ps_num[:, :])
            self.pf(ps_num)
            r = self.headnorm_rstd(hm, C_EPSR)
            self.stt(hm[:, :], hm[:, :], self.vv(l, "hnorm", c), r[:, :], ALU.mult, ALU.mult)
            self.tt(self.ybf[2][:, c, :], hm[:, :], hg[c][:, :], ALU.mult)
            self.fr(hm, r, hg[c])

    def s5(self, l, i):
        slot = self.wload(l, 6)
        us = [None, None]

        def cons(m, ps):
            t = self.wt()
            self.copy(t[:, :], ps[:, :])
            self.copy(self.ybf[3][:, m, :], ps[:, :])
            us[m] = t
        self.dense_fm(slot, self.ub, range(2), cons)
        yield "PREPDONE"
        if "s" not in self.mixers:
            self.fr(*us)
            self.memset(self.ybf[3][:, :, :], 0.0)
            return
        yg = [None, None]
        for c in range(2):
            py = self.ps()
            nmm = [0]

            def group(gg, c=c, py=py, nmm=nmm):
                g = c * 8 + gg
                sl = self.s5ring[self.s5i % 2]
                slb = self.s5ringb[self.s5i % 2]
                self.s5i += 1
                self.load(sl[:, 512:S5W_], self.s5s[g][:, 512:S5W_])
                self.load(slb[:, :], self.s5sb[g])
                pa, pb = self.ps(), self.ps()
                self.mm(pa[:, :], slb[:, 0:128], self.ybf[3][:, c, :])
                self.mm(pb[:, :], slb[:, 128:256], self.ybf[3][:, c, :])
                t1, t2 = self.wt(), self.wt()
                self.tt(t1[:, :], pa[:, :], sl[:, 512:1024], ALU.mult)
                self.tt(t2[:, :], pb[:, :], sl[:, 1024:1536], ALU.mult)
                self.pf(pa, pb)
                yield
                self.tt(t1[:, :], t1[:, :], t2[:, :], ALU.add, eng="pool")
                z = t2
                self.scan(z[:, :], self.s5rho[:, G + g:G + g + 1].bc([128, 512]), t1[:, :], self.s5carry[:, g:g + 1])
                yield
                zs_t = self.wt()
                zc = t1[:, :].m(lambda ap: ap.bitcast(BF16))[:, 0:512]
                zs = zs_t[:, :].m(lambda ap: ap.bitcast(BF16))[:, 0:512]
                self.stt(zc, z[:, :], self.cc(C_SGN), sl[:, 512:1024], ALU.mult, ALU.mult)
                self.stt(zs, z[:, :], -1.0, sl[:, 1024:1536], ALU.mult, ALU.mult)
                yield
                self.mm(py[:, :], slb[:, 256:384], zc, start=(nmm[0] == 0), stop=False)
                self.mm(py[:, :], slb[:, 384:512], zs, start=False, stop=(nmm[0] == 14))
                nmm[0] += 2
                pc = self.ps()
                self.mm(pc[:, 0:1], sl[:, 1536:1664], z[:, 511:512])
                self.copy(self.s5carry[:, g:g + 1], pc[:, 0:1])
                self.pf(pc)
                self.fr(t1, t2, zs_t)
            pend = list(range(8))
            act_ = []
            while pend or act_:
                while pend and len(act_) < 2:
                    act_.append(group(pend.pop(0)))
                for th in list(act_):
                    try:
                        next(th)
                    except StopIteration:
                        act_.remove(th)
                yield "CHAIN"
            y = self.wt()
            self.stt(y[:, :], us[c][:, :], self.vv(l, "sd", c), py[:, :], ALU.mult, ALU.add)
            self.pf(py)
            self.act(y[:, :], y[:, :], AF.Gelu_apprx_tanh)
            yg[c] = y
        self.fr(*us)
        for c in range(2):
            ps = self.ps()
            for kk in range(2):
                self.mm(ps[:, :], self.mats[:, MAT_GLU + kk * 256 + c * 128:MAT_GLU + kk * 256 + (c + 1) * 128], yg[kk][:, :],
                        start=(kk == 0), stop=(kk == 1))
            gt = self.wt()
            self.act(gt[:, :], ps[:, :], AF.Sigmoid, bias=self.vv(l, "sbglu", c))
            self.pf(ps)
            self.tt(self.ybf[3][:, c, :], yg[c][:, :], gt[:, :], ALU.mult)
            self.fr(gt)
        self.fr(*yg)

    def rwkv_prep(self, l, i):
        xm = [None] * 8
        self.xm = xm
        for bi in (3, 4):
            slot = self.wload(l, bi)
            for m in range(4):
                j = (bi - 3) * 4 + m
                ps = self.ps()
                for k in range(8):
                    self.mm(ps[:, :], slot[:, k * 512 + m * 128:k * 512 + (m + 1) * 128], self.ub[:, k, :], start=(k == 0), stop=(k == 7))
                cur, d = self.wt(), self.wt()
                self.copy(cur[:, :], ps[:, :])
                self.pf(ps)
                self.tt(d[:, 1:512], cur[:, 0:511], cur[:, 1:512], ALU.subtract)
                self.tt(d[:, 0:1], self.rhalo[:, j:j + 1], cur[:, 0:1], ALU.subtract)
                self.copy(self.rhalo[:, j:j + 1], cur[:, 511:512], eng="dve")
                self.stt(d[:, :], d[:, :], self.vv(l, "rmu", j), cur[:, :], ALU.mult, ALU.add)
                self.fr(cur)
                xm[j] = d
                yield

    def rwkv(self, l, i):
        xm = self.xm
        if "r" not in self.mixers:
            self.fr(*xm)
            self.memset(self.ybf[1][:, :, :], 0.0)
            return
        r_, k_, v_ = xm[0:2], xm[2:4], xm[4:6]
        x6, x7 = xm[6], xm[7]
        self.act(x6[0:64, :], x6[0:64, :], AF.Tanh)
        self.act(x7[:, :], x7[:, :], AF.Sigmoid)
        lw, a_, gout = [None, None], [None, None], [None, None]
        for c in range(2):
            ps = self.ps()
            self.mm(ps[:, :], self.mats[0:64, MAT_W2A2 + c * 128:MAT_W2A2 + (c + 1) * 128], x6[0:64, :])
            t = self.wt()
            self.act(t[:, :], ps[:, :], AF.Sigmoid, bias=self.vv(l, "rw0", c))
            self.ts(t[:, :], t[:, :], -0.6065306597126334, None, ALU.mult)
            lw[c] = t
            self.pf(ps)
            ps = self.ps()
            self.mm(ps[:, :], self.mats[64:128, MAT_W2A2 + c * 128:MAT_W2A2 + (c + 1) * 128], x6[64:128, :])
            t = self.wt()
            self.act(t[:, :], ps[:, :], AF.Sigmoid, bias=self.vv(l, "ra0", c))
            a_[c] = t
            self.pf(ps)
            ps = self.ps()
            self.mm(ps[:, :], self.mats[:, MAT_G2 + c * 128:MAT_G2 + (c + 1) * 128], x7[:, :])
            t = self.wt()
            self.copy(t[:, :], ps[:, :])
            gout[c] = t
            self.pf(ps)
        self.fr(x6, x7)
        tsl = slice(i * LT, (i + 1) * LT)
        if l == 0:
            for c in range(2):
                self.fw.dma("pool", self.vfirst[c * 128:(c + 1) * 128, tsl], v_[c][:, :])
        else:
            ps = self.ps()
            for kk in range(2):
                self.mm(ps[0:32, :], self.mats[:, MAT_V1 + kk * 32:MAT_V1 + (kk + 1) * 32], v_[kk][:, :], start=(kk == 0), stop=(kk == 1))
            t1 = self.wt()
            self.copy(t1[0:32, :], ps[0:32, :])
            self.pf(ps)
            for c in range(2):
                ps = self.ps()
                self.mm(ps[:, :], self.mats[0:32, MAT_V2 + c * 128:MAT_V2 + (c + 1) * 128], t1[0:32, :])
                s, vf = self.wt(), self.wt()
                self.act(s[:, :], ps[:, :], AF.Sigmoid, bias=self.vv(l, "rv0", c))
                self.pf(ps)
                self.load(vf[:, :], self.vfirst[c * 128:(c + 1) * 128, tsl])
                self.tt(vf[:, :], vf[:, :], v_[c][:, :], ALU.subtract)
                self.tt(vf[:, :], vf[:, :], s[:, :], ALU.mult)
                self.tt(v_[c][:, :], v_[c][:, :], vf[:, :], ALU.add)
                self.fr(s, vf)
            self.fr(t1)
        ypair = []
        for c in range(2):
            r, k, v, a = r_[c], k_[c], v_[c], a_[c]
            kk = self.wt()
            sq = self.wt()
            self.ts(kk[:, :], k[:, :], self.vv(l, "rkk", c), None, ALU.mult)
            self.act(sq[:, :], kk[:, :], AF.Square)
            ps = self.ps()
            self.mm(ps[:, :], self.cc(C_BO, 128), sq[:, :])
            self.act(sq[:, :], ps[:, :], AF.Sqrt)
            self.pf(ps)
            self.ts(sq[:, :], sq[:, :], 1e-12, None, ALU.max)
            self.recip(sq[:, :], sq[:, :])
            self.tt(kk[:, :], kk[:, :], sq[:, :], ALU.mult)
            km = sq
            self.ts(km[:, :], a[:, :], -1.0, self.vv(l, "rka", c), ALU.add, ALU.mult)
            self.stt(km[:, :], km[:, :], 1.0, k[:, :], ALU.add, ALU.mult)
            self.fr(k)
            bon = self.wt()
            self.stt(bon[:, :], r[:, :], self.vv(l, "rrk", c), km[:, :], ALU.mult, ALU.mult)
            ps = self.ps()
            self.mm(ps[:, :], self.cc(C_BO, 128), bon[:, :])
            self.tt(bon[:, :], ps[:, :], v[:, :], ALU.mult)
            self.pf(ps)
            r3 = lambda t: t[:, :].re("p (c t) -> p c t", t=128)
            gc, gx, e1, e2 = self.wt(), self.wt(), self.wt(), self.wt()
            self.scan(gc[:, :], self.cc(C_RST, 512), lw[c][:, :], 0.0)
            self.tt(gx[:, :], gc[:, :], lw[c][:, :], ALU.subtract)
            ar, art = self.ar[c], self.art[c]
            self.act(e1[:, :], gx[:, :], AF.Exp)
            self.stt(art[:, 0, :], kk[:, :], -1.0, e1[:, :], ALU.mult, ALU.mult)
            self.act(e1[:, :], gc[:, :], AF.Exp)
            self.tt(art[:, 1, :], r[:, :], e1[:, :], ALU.mult)
            ein = lw[c]
            self.copy(ein[:, :], e1[:, :], eng="pool")
            gmid = r3(gc)[:, :, 63:64].bc([128, 4, 128])
            self.tt(r3(gx), r3(gx), gmid, ALU.subtract)
            self.act(e1[:, :], gx[:, :], AF.Exp)
            self.stt(ar[:, 0, :], kk[:, :], -1.0, e1[:, :], ALU.mult, ALU.mult)
            self.tt(r3(e2), r3(gc), gmid, ALU.subtract)
            self.act(e1[:, :], e2[:, :], AF.Exp)
            self.tt(ar[:, 1, :], r[:, :], e1[:, :], ALU.mult)
            self.act(e1[:, :], e2[:, :], AF.Exp, scale=-1.0)
            bv, bt, kt = r, gx, e2
            self.tt(bv[:, :], kk[:, :], a[:, :], ALU.mult)
            self.tt(bt[:, :], bv[:, :], e1[:, :], ALU.mult)
            self.tt(kt[:, :], km[:, :], e1[:, :], ALU.mult)
            self.fr(kk, a)
            glast = r3(gc)[:, :, 127:128].bc([128, 4, 128])
            self.tt(r3(e1), glast, r3(gc), ALU.subtract)
            self.act(e1[:, :], e1[:, :], AF.Exp)
            bh, kh = bv, km
            self.tt(bh[:, :], bv[:, :], e1[:, :], ALU.mult)
            self.tt(kh[:, :], km[:, :], e1[:, :], ALU.mult)
            self.to_tok_z(bh, self.bz)
            self.to_tok_z(kh, self.kz)
            vtok = gc
            self.to_tok(v, vtok)
            self.fr(bh, kh, v, e1)
            py = self.ps()
            gam2 = ein[:, :].re("p (c t) -> p c t", t=128)[:, :, 127]
            done = {}
            consumed = [self.blks[0] - 1]

            def chainA(blk, hh):
                bsl = slice(blk * 128, (blk + 1) * 128)
                hp = slice(64 * hh, 64 * hh + 64)
                m4, T = self.wt(), self.wt()
                PRt = self.prt[(blk % 2) * 2 + hh]
                PR = PRt[:, :]
                PP = PRt[:, :].m(lambda ap: ap.bitcast(F32))
                Tb = T[:, 0:128]
                pa, pn = self.ps(), self.ps()
                self.mm(pa[:, 0:256], bt[hp, bsl], ar[hp, :, bsl])
                self.mm(pa[:, 256:512], kt[hp, bsl], ar[hp, :, bsl])
                self.mm(pn[:, 0:128], ar[hp, 0, bsl], bt[hp, bsl])
                self.tt(m4[:, :].re("p (a m) -> p a m", a=2), pa[:, :].re("p (a m) -> p a m", a=2),
                        self.cc(C_MSU, 256).m(lambda ap: ap.unsqueeze(1)).bc([128, 2, 256]), ALU.mult)
                self.tt(T[:, 256:384], pn[:, 0:128], self.cc(C_MSL, 128), ALU.mult)
                self.pf(pa, pn)
                yield
                self.copy(PR[:, 0:128], T[:, 256:384])
                self.copy(PR[:, 128:256], m4[:, 0:128], eng="dve")
                self.tt(Tb, m4[:, 0:128], self.cc(C_ID, 128), ALU.add, eng="pool")
                yield
                NIT = 6
                for it in range(NIT + 1):
                    cur = (it % 2) * 256
                    nxt = 256 - cur
                    if it < NIT:
                        pq = self.ps()
                        self.mm(pq[:, 0:128], PR[:, cur + 128:cur + 256], PR[:, cur:cur + 128])
                        if it < NIT - 1:
                            self.mm(pq[:, 128:256], PR[:, cur:cur + 128], PR[:, cur + 128:cur + 256])
                            self.copy(PR[:, nxt:nxt + 256], pq[:, 0:256])
                        else:
                            self.copy(PR[:, nxt:nxt + 128], pq[:, 0:128])
                        self.pf(pq)
                    if it >= 1:
                        pq = self.ps()
                        self.mm(pq[:, 0:128], PP[:, cur:cur + 128], Tb)
                        self.tt(Tb, Tb, pq[:, 0:128], ALU.add)
                        self.pf(pq)
                    yield
                done[(blk, hh)] = (m4, T)

            def phaseB():
                for blk in self.blks:
                    bsl = slice(blk * 128, (blk + 1) * 128)
                    while (blk, 0) not in done or (blk, 1) not in done:
                        yield
                    for hh in range(2):
                        hc = slice(blk * 128 + hh * 64, blk * 128 + hh * 64 + 64)
                        ST = self.strw[2 * c + hh]
                        m4, T = done[(blk, hh)]
                        pz = self.ps()
                        self.mm(pz[:, 0:64], m4[:, 256:384], vtok[:, hc], start=True, stop=False)
                        self.mm(pz[:, 0:64], art[:, 0, bsl], ST[:, :], start=False, stop=True)
                        self.copy(T[:, 128:192], pz[:, 0:64])
                        self.pf(pz)
                    yield
                    for hh in range(2):
                        m4, T = done[(blk, hh)]
                        pz = self.ps()
                        self.mm(pz[:, 0:64], T[:, 0:128], T[:, 128:192])
                        self.copy(T[:, 192:256], pz[:, 0:64])
                        self.pf(pz)
                    yield
                    for hh in range(2):
                        hp = slice(64 * hh, 64 * hh + 64)
                        hc = slice(blk * 128 + hh * 64, blk * 128 + hh * 64 + 64)
                        ST = self.strw[2 * c + hh]
                        m4, T = done[(blk, hh)]
                        self.mm(py[hp, bsl], ST[:, :], art[:, 1, bsl], start=True, stop=False)
                        self.mm(py[hp, bsl], T[:, 192:256], m4[:, 128:256], start=False, stop=False)
                        self.mm(py[hp, bsl], vtok[:, hc], m4[:, 384:512], start=False, stop=True)
                        pst = self.ps()
                        self.mm(pst[:, 0:64], self.bz[hh][:, bsl], T[:, 192:256], start=True, stop=False)
                        self.mm(pst[:, 0:64], self.kz[hh][:, bsl], vtok[:, hc], start=False, stop=True)
                        self.stt(ST[:, :], ST[:, :], gam2[:, blk:blk + 1], pst[:, 0:64], ALU.mult, ALU.add)
                        self.pf(pst)
                        self.fr(m4, T)
                    consumed[0] = blk
                    yield

            pending = [(blk, hh) for blk in self.blks for hh in range(2)]
            threads = [phaseB()]
            nA = [0]

            def start_more():
                while pending and pending[0][0] <= consumed[0] + 2:
                    blk, hh = pending.pop(0)
                    threads.append(chainA(blk, hh))
            start_more()
            while threads:
                for th in list(threads):
                    try:
                        next(th)
                    except StopIteration:
                        threads.remove(th)
                start_more()
                yield "CHAIN"
            self.fr(bt, kt, vtok, ein)
            y = self.wt()
            self.copy(y[:, :], py[:, :])
            self.pf(py)
            ps = self.ps()
            self.mm(ps[:, :], self.cc(C_BO, 128), y[:, :])
            self.stt(y[:, :], ps[:, :], -1.0 / 64, y[:, :], ALU.mult, ALU.add)
            self.pf(ps)
            sq = self.wt()
            self.act(sq[:, :], y[:, :], AF.Square)
            ps = self.ps()
            self.mm(ps[:, :], self.cc(C_BO, 128), sq[:, :])
            self.act(sq[:, :], ps[:, :], AF.Sqrt, bias=self.cc(C_EPSG), scale=1.0 / 64)
            self.pf(ps)
            self.recip(sq[:, :], sq[:, :])
            self.tt(y[:, :], y[:, :], sq[:, :], ALU.mult)
            self.ts(y[:, :], y[:, :], self.vv(l, "rgnw", c), self.vv(l, "rgnb", c), ALU.mult, ALU.add)
            self.tt(y[:, :], y[:, :], bon[:, :], ALU.add)
            self.tt(self.ybf[1][:, c, :], y[:, :], gout[c][:, :], ALU.mult)
            self.fr(y, sq, bon, gout[c])

    def merge_branch(self, l, b):
        ga = self.wload(l, 8 + 2 * b)
        brs = self.wload(l, 16 + b // 2)
        for m in range(8):
            if m == 4:
                ga = self.wload(l, 9 + 2 * b)
            pg = self.ps()
            for k in range(8):
                self.mm(pg[:, :], ga[:, k * 512 + (m % 4) * 128:k * 512 + (m % 4 + 1) * 128], self.ub[:, k, :], start=(k == 0), stop=(k == 7))
            gt = self.wt()
            self.act(gt[:, :], pg[:, :], AF.Sigmoid, bias=self.vv(l, "bgate", b * 8 + m))
            self.pf(pg)
            pb = self.ps()
            for kk in range(2):
                sidx = (b % 2) * 4 + kk * 2 + m // 4
                j = m % 4
                self.mm(pb[:, :], brs[:, sidx * 512 + j * 128:sidx * 512 + (j + 1) * 128], self.ybf[b][:, kk, :], start=(kk == 0), stop=(kk == 1))
            if b == 0:
                self.tt(self.mergedb[:, m, :], gt[:, :], pb[:, :], ALU.mult)
            else:
                self.tt(gt[:, :], gt[:, :], pb[:, :], ALU.mult)
                self.tt(self.mergedb[:, m, :], self.mergedb[:, m, :], gt[:, :], ALU.add, eng="pool")
            self.pf(pb)
            self.fr(gt)
            yield

    def bg_convert(self, l):
        for b in range(NBLK):
            for j in range(8):
                t = self.wt()
                cb = self.cvb[(b * 8 + j) % 2]
                self.load(t[:, :], self.wpack[l * NBLK + b][:, j * 512:(j + 1) * 512])
                self.copy(cb[:, :], t[:, :], eng="pool")
                self.fr(t)
                self.fw.dma("pool", self.wbf[l][b][:, j * 512:(j + 1) * 512], cb[:, :])
                yield

    def bg_step(self, n=1):
        if self.bg is None:
            return
        for _ in range(n):
            try:
                next(self.bg)
            except StopIteration:
                self.bg = None
                return

    def with_side(self, main, side):
        side_live = side is not None
        for tok in main:
            if tok == "CHAIN":
                self.bg_step()
            if tok == "CHAIN" and side_live:
                try:
                    next(side)
                except StopIteration:
                    side_live = False
        if side_live:
            for _ in side:
                pass

    def interleave(self, mains, side):
        live = []
        for g in mains:
            for tok in g:
                if tok == "PREPDONE":
                    live.append(g)
                    break
        live.append(side)
        while live:
            self.bg_step()
            for g in list(live):
                try:
                    next(g)
                except StopIteration:
                    live.remove(g)

    def merge_out(self, l, i):
        self.proj_norm_res(l, [18, 19], lambda k: self.mergedb[:, k, :], 8, 1, i)

    def proj_norm_res(self, l, blocks, rhs_k, nk, nidx, i, ffn=False):
        mo = [None] * 8
        pss = self.ps()
        for c in range(2):
            if not ffn:
                slot = self.wload(l, blocks[c])
                pm = [self.ps() for _ in range(4)]
                for m in range(4):
                    for k in range(8):
                        self.mm(pm[m][:, :], slot[:, k * 512 + m * 128:k * 512 + (m + 1) * 128], rhs_k(k), start=(k == 0), stop=(k == 7))
            else:
                pm = [self.ps() for _ in range(4)]
                for g in range(3):
                    slot = self.wload(l, blocks[c * 3 + g])
                    for s in range(8):
                        k = 8 * g + s
                        if k >= 22:
                            continue
                        for m in range(4):
                            self.mm(pm[m][:, :], slot[:, s * 512 + m * 128:s * 512 + (m + 1) * 128], rhs_k(k), start=(k == 0), stop=(k == 21))
            for m in range(4):
                t = self.wt()
                self.copy(t[:, :], pm[m][:, :])
                sq = self.sq_tile()
                self.act(sq[:, :], pm[m][:, :], AF.Square)
                self.mm(pss[:, :], self.onesb[:, :], sq[:, :], start=(c == 0 and m == 0), stop=(c == 1 and m == 3))
                mo[c * 4 + m] = t
            self.pf(*pm)
        r = self.rstd_from(pss, 1.0 / D, C_EPSR)
        self.pf(pss)
        for k in range(8):
            self.stt(mo[k][:, :], mo[k][:, :], self.vv(l, "norms", nidx * 8 + k), r[:, :], ALU.mult, ALU.mult)
            self.tt(self.hT[:, k, :], self.hT[:, k, :], mo[k][:, :], ALU.add, eng="pool")
        self.fr(r, *mo)
        if i == 0:
            self.memset(self.hT[:, :, 0:PADF], 0.0, eng="pool")

    def ffn(self, l, i):
        self.rmsnorm(lambda k: self.hT[:, k, :], l, 2, lambda k: self.ub[:, k, :])
        acts = [self.wt() for _ in range(11)]
        actv = lambda j: acts[j // 2][:, :].m(lambda ap: ap.bitcast(BF16))[:, (j % 2) * 512:(j % 2 + 1) * 512]
        for bi in range(11):
            slot = self.wload(l, 20 + bi)
            conv = [None] * 4

            def cons(m, ps, bi=bi):
                ch = bi * 4 + m
                acc = self.wt()
                w = lambda j: self.vv(l, "fconvw", ch * 3 + j)
                self.act(acc[:, :], ps[:, :], AF.Identity, bias=self.vv(l, "fconvb", ch), scale=w(2))
                self.stt(acc[:, 1:512], ps[:, 0:511], w(1), acc[:, 1:512], ALU.mult, ALU.add)
                self.stt(acc[:, 2:512], ps[:, 0:510], w(0), acc[:, 2:512], ALU.mult, ALU.add)
                self.stt(acc[:, 0:1], self.fhalo[:, ch, 1:2], w(1), acc[:, 0:1], ALU.mult, ALU.add)
                self.stt(acc[:, 0:2], self.fhalo[:, ch, 0:2], w(0), acc[:, 0:2], ALU.mult, ALU.add)
                self.copy(self.fhalo[:, ch, :], ps[:, 510:512], eng="dve")
                conv[m] = acc
            self.dense_fm(slot, self.ub, range(4), cons)
            for m in range(2):
                self.act(conv[m][:, :], conv[m][:, :], AF.Gelu_apprx_tanh)
                self.tt(actv(2 * bi + m), conv[m][:, :], conv[2 + m][:, :], ALU.mult, eng="pool")
            self.fr(*conv)
        self.proj_norm_res(l, list(range(31, 37)), lambda k: actv(k), 22, 3, i, ffn=True)
        self.fr(*acts)

    def build(self):
        NT, L = self.NT, self.L
        self.gts = []
        self.prologue()
        for l in range(L):
            if self.bg is not None:
                for _ in self.bg:
                    pass
                self.bg = None
            if l + 1 < L:
                self.bg = self.bg_convert(l + 1)
            self.layer_prep(l)
            src = self.xT if l == 0 else (self.hA if l % 2 == 1 else self.hB)
            dst = self.out_d if l == L - 1 else (self.hA if l % 2 == 0 else self.hB)
            for i in range(NT):
                tsl = slice(i * LT, (i + 1) * LT)
                self.blks = [3] if i == 0 else [0, 1, 2, 3]
                self.load(self.hT[:, :, :], src[:, tsl].re("(k p) t -> p k t", p=128))
                self.rmsnorm(lambda k: self.hT[:, k, :], l, 0, lambda k: self.ub[:, k, :])
                self.with_side(self.mlstm(l, i), self.rwkv_prep(l, i))
                self.with_side(self.rwkv(l, i), self.merge_branch(l, 0))
                self.interleave([self.hgrn2(l, i), self.s5(l, i)], self.merge_branch(l, 1))
                self.with_side(self.merge_branch(l, 2), None)
                self.with_side(self.merge_branch(l, 3), None)
                self.merge_out(l, i)
                self.ffn(l, i)
                self.fw.dma("pool", dst[:, tsl].re("(k p) t -> p k t", p=128), self.hT[:, :, :])
        self.fw.wait_all("pool", [self.out_d])
        self.fw.emit()
        return self.nc


_CACHE = {}


def run(inputs, NT, L, ncores, mixers="mrhs", debug=False):
    hp = host_pack(inputs, L)
    x = np.asarray(inputs["x"], np.float32)
    meta = np.asarray(inputs["meta"], np.float32)
    key = (NT, L, mixers, debug)
    prog = Prog(NT, L, mixers, debug)
    nc = prog.build()
    in_maps = []
    for b in range(ncores):
        d = dict(hp)
        d["xT"] = host_x(x[b], meta, NT)
        in_maps.append(d)
    res = run_bass_kernel_spmd(nc, in_maps, core_ids=list(range(ncores)))
    outs = []
    for b in range(ncores):
        oT = np.asarray(res.results[b]["outT"])
        outs.append(np.ascontiguousarray(oT[:, LT:].T))
    return np.stack(outs, 0), res, prog


def kernel(**inputs):
    out, _, _ = run(inputs, 17, 4, 8)
    return out.astype(np.float32)
```

```python
import math
import numpy as np
import concourse.bass as bass
import concourse.mybir as mybir
from concourse.bass_utils import run_bass_kernel_spmd

F32 = mybir.dt.float32
F32R = mybir.dt.float32r
BF16 = mybir.dt.bfloat16
I32 = mybir.dt.int32
ALU = mybir.AluOpType
AF = mybir.ActivationFunctionType

D = 1024
LT = 512
PADF = 496
NMETA = 16
G = 16
NBLK = 37
RMS_EPS = 1e-6
GN_EPS = 64e-5
TWO_PI = 2.0 * math.pi
S5W_ = 256 + 256 + 1024 + 128

import os as _os
NWR = 2
SAME_ENGINE_SYNC = True


class View:
    __slots__ = ("buf", "ap")

    def __init__(self, buf, ap):
        self.buf = buf
        self.ap = ap

    def __getitem__(self, idx):
        return View(self.buf, self.ap[idx])

    def m(self, f):
        return View(self.buf, f(self.ap))

    def re(self, pat, **kw):
        return View(self.buf, self.ap.rearrange(pat, **kw))

    def bc(self, shape):
        return View(self.buf, self.ap.broadcast_to(list(shape)))


class Buf:
    __slots__ = ("t", "name", "w", "r")

    def __init__(self, t, name):
        self.t = t
        self.name = name
        self.w = None
        self.r = {}

    def __getitem__(self, idx):
        return View(self, self.t[idx])


class Eng:
    def __init__(self, fw, name, h):
        self.name = name
        self.h = h
        self.sem = fw.new_sem("e_" + name)
        self.count = 0
        self.known = {}
        self.prog = []
        self.dma_ring = None
        self.dma_last = None
        self.dma_i = 0


def _ap(x):
    return x.ap if isinstance(x, View) else x


def _bufs(xs):
    out = []
    for x in xs:
        if isinstance(x, View):
            out.append(x.buf)
        elif isinstance(x, Buf):
            out.append(x)
    return out


class FW:
    def __init__(self, nc, n_dma_sems=16):
        self.nc = nc
        self.sems = {}
        self.semvals = {}
        self.E = {}
        for name, h in (("pe", nc.tensor), ("dve", nc.vector), ("act", nc.scalar),
                        ("pool", nc.gpsimd), ("sp", nc.sync)):
            self.E[name] = Eng(self, name, h)
        for qn in ("sp", "pool", "act"):
            e = self.E[qn]
            e.dma_ring = [self.new_sem("d_%s_%d" % (qn, i)) for i in range(n_dma_sems)]
            e.dma_last = [None] * n_dma_sems
        self.ninstr = 0

    def new_sem(self, name):
        s = self.nc.alloc_semaphore(name)
        self.sems[name] = s
        self.semvals[name] = 0
        return name

    def sbuf(self, name, shape, dtype=F32):
        return Buf(self.nc.alloc_sbuf_tensor(name, list(shape), dtype), name)

    def psum(self, name, shape, dtype=F32):
        return Buf(self.nc.alloc_psum_tensor(name, list(shape), dtype), name)

    def dram(self, name, shape, dtype=F32, kind="Internal"):
        return Buf(self.nc.dram_tensor(name, list(shape), dtype, kind=kind), name)

    def _collect(self, eng, reads, writes):
        waits = {}

        def add(ev):
            k, v = ev
            if waits.get(k, 0) < v:
                waits[k] = v
        for b in reads:
            if b.w is not None:
                if b.w[0] == eng.sem and not SAME_ENGINE_SYNC:
                    continue
                add(b.w)
        for b in writes:
            if b.w is not None and b.w[0] != eng.sem:
                add(b.w)
            for k, v in b.r.items():
                if k != eng.sem:
                    add((k, v))
        out = []
        for k, v in waits.items():
            if eng.known.get(k, 0) >= v:
                continue
            eng.known[k] = v
            out.append((k, v))
        return out

    def _emit_waits(self, eng, waits):
        for k, v in waits:
            s = self.sems[k]
            eng.prog.append(lambda h=eng.h, s=s, v=v: h.wait_ge(s, v))

    def _mark(self, ev, reads, writes):
        for b in reads:
            if b.r.get(ev[0], 0) < ev[1]:
                b.r[ev[0]] = ev[1]
        for b in writes:
            b.w = ev
            b.r = {}

    def op(self, engname, fn, reads=(), writes=()):
        eng = self.E[engname]
        reads = _bufs(reads)
        writes = _bufs(writes)
        self._emit_waits(eng, self._collect(eng, reads, writes))
        eng.count += 1
        ev = (eng.sem, eng.count)
        s = self.sems[eng.sem]
        eng.prog.append(lambda h=eng.h, fn=fn, s=s: fn(h).then_inc(s, 1))
        self._mark(ev, reads, writes)
        self.ninstr += 1
        return ev

    def dma(self, qname, out, in_, extra_reads=(), extra_writes=(), **kw):
        eng = self.E[qname]
        reads = _bufs([in_] + list(extra_reads))
        writes = _bufs([out] + list(extra_writes))
        i = eng.dma_i % len(eng.dma_ring)
        eng.dma_i += 1
        semk = eng.dma_ring[i]
        waits = self._collect(eng, reads, writes)
        prev = eng.dma_last[i]
        if prev is not None and eng.known.get(prev[0], 0) < prev[1]:
            eng.known[prev[0]] = prev[1]
            waits.append(prev)
        self._emit_waits(eng, waits)
        self.semvals[semk] += 16
        ev = (semk, self.semvals[semk])
        eng.dma_last[i] = ev
        s = self.sems[semk]
        o, a = _ap(out), _ap(in_)
        eng.prog.append(lambda h=eng.h, o=o, a=a, s=s, kw=kw: h.dma_start(out=o, in_=a, **kw).then_inc(s, 16))
        self._mark(ev, reads, writes)
        self.ninstr += 1
        return ev

    def wait_all(self, engname, bufs):
        eng = self.E[engname]
        self._emit_waits(eng, self._collect(eng, _bufs(bufs), ()))

    def emit(self):
        nc = self.nc
        with nc.Block() as block:
            @block.tensor
            def _(e):
                for f in self.E["pe"].prog:
                    f()

            @block.vector
            def _(e):
                for f in self.E["dve"].prog:
                    f()

            @block.scalar
            def _(e):
                for f in self.E["act"].prog:
                    f()

            @block.gpsimd
            def _(e):
                for f in self.E["pool"].prog:
                    f()

            @block.sync
            def _(e):
                for f in self.E["sp"].prog:
                    f()


M0, R0, H0, S0 = 0, 1032, 2056, 3080


def _blk_k1024(Wc):
    return Wc.reshape(8, 128, 512).transpose(1, 0, 2).reshape(128, 4096)


class VecLayout:
    def __init__(self):
        self.off = {}
        self.n = 0

    def add(self, name, ncols):
        self.off[name] = self.n
        self.n += ncols


def vec_layout():
    v = VecLayout()
    v.add("norms", 32)
    v.add("bgate", 32)
    v.add("mconvw", 16)
    v.add("mconvb", 4)
    v.add("mig", 2)
    v.add("mfg", 2)
    v.add("mnorm", 2)
    v.add("rmu", 8)
    for n in ("rw0", "ra0", "rkk", "rka", "rrk", "rgnw", "rgnb", "rv0", "hnorm", "sd", "sbglu"):
        v.add(n, 2)
    v.add("fconvw", 132)
    v.add("fconvb", 44)
    return v


MAT_W2A2, MAT_G2, MAT_V1, MAT_V2, MAT_GLU, MAT_N = 0, 256, 512, 576, 832, 1344

C_ID, C_ONES, C_BO, C_MSU, C_MU, C_MSL, C_M2, C_RST, C_SGN, C_SWAP, C_TAU, C_N = (
    0, 128, 256, 384, 512, 640, 768, 896, 1408, 1409, 1537, 2049 + 8)
C_EPSR, C_EPSG, C_ONE, C_HPI, C_ZERO, C_TINY = 2049, 2050, 2051, 2052, 2053, 2054


def build_consts():
    c = np.zeros((128, C_N), np.float32)
    p = np.arange(128)
    c[:, C_ID:C_ID + 128] = np.eye(128)
    c[:, C_ONES:C_ONES + 128] = 1.0
    c[:, C_BO:C_BO + 128] = (p[:, None] // 64 == p[None, :] // 64)
    same = np.ones((128, 128), bool)
    c[:, C_MSU:C_MSU + 128] = same & (p[:, None] < p[None, :])
    c[:, C_MU:C_MU + 128] = same & (p[:, None] <= p[None, :])
    c[:, C_MSL:C_MSL + 128] = same & (p[None, :] < p[:, None])
    c[:, C_M2:C_M2 + 128] = ((p[:, None] % 64) <= (p[None, :] % 64))
    rst = np.ones(512, np.float32)
    rst[::128] = 0.0
    c[:, C_RST:C_RST + 512] = rst[None, :]
    c[:, C_SGN] = np.where(p < 64, 1.0, -1.0)
    c[:, C_SWAP:C_SWAP + 128] = (p[None, :] == (p[:, None] + 64) % 128)
    c[:, C_TAU:C_TAU + 512] = np.arange(1, 513, dtype=np.float32)[None, :]
    c[:, C_EPSR] = RMS_EPS
    c[:, C_EPSG] = GN_EPS
    c[:, C_ONE] = 1.0
    c[:, C_HPI] = math.pi / 2
    c[:, C_ZERO] = 0.0
    c[:, C_TINY] = 1e-12
    return c


def colvec(a, n):
    return np.ascontiguousarray(a.reshape(n, 128).T)


def ffn_chunk_order():
    order = []
    for i in range(11):
        order += [2 * i, 2 * i + 1, 22 + 2 * i, 22 + 2 * i + 1]
    return order


def host_pack(inp, L):
    f = lambda k: np.asarray(inp[k], np.float32)
    w_in, w_gate, w_branch, w_out, f_up, f_down = f("w_in"), f("w_gate"), f("w_branch"), f("w_out"), f("f_up"), f("f_down")
    wpack = np.zeros((L, NBLK, 128, 4096), np.float32)
    for l in range(L):
        Wi = w_in[l]
        blks = []
        blks.append(Wi[:, M0:M0 + 512])
        blks.append(np.concatenate([Wi[:, M0 + 768:M0 + 1024], Wi[:, H0 + 768:H0 + 1024]], 1))
        ig = np.concatenate([np.repeat(Wi[:, M0 + 1024 + h:M0 + 1025 + h], 64, 1) for h in range(4)], 1)
        fg = np.concatenate([np.repeat(Wi[:, M0 + 1028 + h:M0 + 1029 + h], 64, 1) for h in range(4)], 1)
        blks.append(np.concatenate([ig, fg], 1))
        blks.append(Wi[:, R0:R0 + 512])
        blks.append(Wi[:, R0 + 512:R0 + 1024])
        blks.append(Wi[:, H0:H0 + 512])
        blks.append(np.concatenate([Wi[:, S0:S0 + 256], np.zeros((1024, 256), np.float32)], 1))
        blks.append(np.concatenate([Wi[:, M0 + 512:M0 + 768], Wi[:, H0 + 512:H0 + 768]], 1))
        for i, b in enumerate(blks):
            wpack[l, i] = _blk_k1024(b)
        for b in range(4):
            for c in range(2):
                wpack[l, 8 + 2 * b + c] = _blk_k1024(w_gate[l, b][:, c * 512:(c + 1) * 512])
        for b in range(4):
            blk = wpack[l, 16 + b // 2].reshape(128, 8, 512)
            for kk in range(2):
                for m in range(8):
                    s = (b % 2) * 4 + kk * 2 + m // 4
                    j = m % 4
                    blk[:, s, j * 128:(j + 1) * 128] = w_branch[l, b][kk * 128:(kk + 1) * 128, m * 128:(m + 1) * 128]
        for c in range(2):
            wpack[l, 18 + c] = _blk_k1024(w_out[l][:, c * 512:(c + 1) * 512])
        for i in range(11):
            cols = np.concatenate([f_up[l][:, 256 * i:256 * i + 256], f_up[l][:, 2816 + 256 * i:2816 + 256 * i + 256]], 1)
            wpack[l, 20 + i] = _blk_k1024(cols)
        for c in range(2):
            for g in range(3):
                blk = wpack[l, 31 + c * 3 + g].reshape(128, 8, 512)
                for s in range(8):
                    k = 8 * g + s
                    if k < 22:
                        blk[:, s, :] = f_down[l][k * 128:(k + 1) * 128, c * 512:(c + 1) * 512]
    VL = vec_layout()
    vecs = np.zeros((128, L, VL.n), np.float32)
    order = ffn_chunk_order()
    for l in range(L):
        def put(name, arr):
            vecs[:, l, VL.off[name]:VL.off[name] + arr.shape[1]] = arr
        put("norms", np.concatenate([colvec(f("norms")[l, j], 8) for j in range(4)], 1))
        put("bgate", np.concatenate([colvec(f("b_gate")[l, b], 8) for b in range(4)], 1))
        cw = f("m_conv_w")[l]
        put("mconvw", np.stack([colvec(cw[j], 4) for j in range(4)], 2).reshape(128, 16))
        put("mconvb", colvec(f("m_conv_b")[l], 4))
        gb = f("m_gate_b")[l]
        put("mig", colvec(np.repeat(gb[0], 64), 2))
        put("mfg", colvec(np.repeat(gb[1], 64), 2))
        put("mnorm", colvec(f("m_norm")[l], 2))
        put("rmu", colvec(f("r_mu")[l], 8))
        for n, k in (("rw0", "r_w0"), ("ra0", "r_a0"), ("rkk", "r_kk"), ("rka", "r_ka"), ("rrk", "r_rk"),
                     ("rgnw", "r_gn_w"), ("rgnb", "r_gn_b"), ("hnorm", "h_norm"), ("sd", "s_d"), ("sbglu", "s_b_glu")):
            put(n, colvec(f(k)[l], 2))
        if l >= 1:
            put("rv0", colvec(f("r_v0")[l - 1], 2))
        fcw = f("f_conv_w")[l]
        fcb = f("f_conv_b")[l]
        put("fconvw", np.stack([colvec(fcw[j], 44)[:, order] for j in range(3)], 2).reshape(128, 132))
        put("fconvb", colvec(fcb, 44)[:, order])
    hlb = np.ascontiguousarray(np.stack([colvec(f("h_lb")[l], 2) for l in range(L)], 2))
    mats = np.zeros((L, 128, MAT_N), np.float32)
    for l in range(L):
        mats[l, 0:64, MAT_W2A2:MAT_W2A2 + 256] = f("r_w2")[l]
        mats[l, 64:128, MAT_W2A2:MAT_W2A2 + 256] = f("r_a2")[l]
        mats[l, :, MAT_G2:MAT_G2 + 256] = f("r_g2")[l]
        if l >= 1:
            mats[l, :, MAT_V1:MAT_V1 + 64] = f("r_v1")[l - 1].reshape(2, 128, 32).transpose(1, 0, 2).reshape(128, 64)
            mats[l, 0:32, MAT_V2:MAT_V2 + 256] = f("r_v2")[l - 1]
        mats[l, :, MAT_GLU:MAT_GLU + 512] = f("s_w_glu")[l].reshape(2, 128, 256).transpose(1, 0, 2).reshape(128, 512)
    s5row = np.zeros((L, 3, 1024), np.float32)
    s5col = np.zeros((L, 128, 3 * G), np.float32)
    s5b = np.zeros((L, 128, 2, G, 64), np.float32)
    s5c = np.zeros((L, G, 128, 256), np.float32)
    for l in range(L):
        are, aim, ls = f("s_a_re")[l], f("s_a_im")[l], f("s_log_step")[l]
        s5row[l, 0] = are.reshape(-1)
        s5row[l, 1] = aim.reshape(-1)
        s5row[l, 2] = np.repeat(ls, 64)
        s5col[l, :, 0:G] = np.concatenate([are.T, are.T], 0)
        s5col[l, :, G:2 * G] = np.concatenate([aim.T, aim.T], 0)
        s5col[l, :, 2 * G:3 * G] = np.broadcast_to(ls[None, :], (128, G))
        bre, bim = f("s_b_re")[l], f("s_b_im")[l]
        cre, cim = f("s_c_re")[l], f("s_c_im")[l]
        for g in range(G):
            r0 = 16 * (g % 8)
            s5b[l, r0:r0 + 16, 0, g, :] = bre[g].T
            s5b[l, r0:r0 + 16, 1, g, :] = bim[g].T
            s5c[l, g, 0:64, r0:r0 + 16] = cre[g].T
            s5c[l, g, 64:128, r0:r0 + 16] = cim[g].T
            s5c[l, g, 0:64, 128 + r0:128 + r0 + 16] = cim[g].T
            s5c[l, g, 64:128, 128 + r0:128 + r0 + 16] = cre[g].T
    return {"wpack": wpack.reshape(L * NBLK, 128, 4096), "vecs": vecs.reshape(128, L * VL.n), "hlb": hlb.reshape(128, 2 * L),
            "mats": mats, "s5row": s5row, "s5col": s5col, "s5b": s5b.reshape(L, 128, 2 * G * 64), "s5c": s5c,
            "consts": build_consts()}


def host_x(x_b, meta, NT):
    TP = NT * LT
    xT = np.zeros((D, TP), np.float32)
    xT[:, PADF:PADF + NMETA] = np.asarray(meta, np.float32).T
    xT[:, LT:] = np.asarray(x_b, np.float32).T
    return xT


class Prog:
    def __init__(self, NT, L, mixers="mrhs", debug=False):
        self.NT, self.L, self.mixers = NT, L, mixers
        self.TP = NT * LT
        nc = bass.Bass("TRN2", target_bir_lowering=False)
        self.nc = nc
        fw = FW(nc)
        self.fw = fw
        VL = vec_layout()
        self.VL = VL
        TP = self.TP
        self.xT = fw.dram("xT", [D, TP], kind="ExternalInput")
        self.wpack = fw.dram("wpack", [L * NBLK, 128, 4096], kind="ExternalInput")
        self.vecs_d = fw.dram("vecs", [128, L * VL.n], kind="ExternalInput")
        self.hlb_d = fw.dram("hlb", [128, 2 * L], kind="ExternalInput")
        self.mats_d = fw.dram("mats", [L, 128, MAT_N], kind="ExternalInput")
        self.s5row_d = fw.dram("s5row", [L, 3, 1024], kind="ExternalInput")
        self.s5col_d = fw.dram("s5col", [L, 128, 3 * G], kind="ExternalInput")
        self.s5b_d = fw.dram("s5b", [L, 128, 2 * G * 64], kind="ExternalInput")
        self.s5c_d = fw.dram("s5c", [L, G, 128, 256], kind="ExternalInput")
        self.consts_d = fw.dram("consts", [128, C_N], kind="ExternalInput")
        self.out_d = fw.dram("outT", [D, TP], kind="ExternalOutput")
        self.wbf = [fw.dram("wbf%d" % l, [NBLK, 128, 4096], BF16) for l in range(L)]
        self.hA = fw.dram("hA", [D, TP])
        self.hB = fw.dram("hB", [D, TP])
        self.vfirst = fw.dram("vfirst", [256, TP])
        S5W = 256 + 256 + 1024 + 128
        self.S5W = S5W
        self.s5s = fw.dram("s5s", [G, 128, S5W])
        self.s5sb = fw.dram("s5sb", [G, 128, 512], BF16)
        self.consts = fw.sbuf("consts_s", [128, C_N])
        self.vecs = fw.sbuf("vecs_s", [128, L * VL.n])
        self.mats = fw.sbuf("mats_s", [128, MAT_N])
        self.lbs = fw.sbuf("lbs_s", [128, 6 * L])
        self.hT = fw.sbuf("hT", [128, 8, LT])
        self.ub = fw.sbuf("ub", [128, 8, LT], BF16)
        self.wring = [fw.sbuf("wr%d" % i, [128, 4096], BF16) for i in range(NWR)]
        self.wi = 0
        self.ybf = [fw.sbuf("ybf%d" % i, [128, 2, LT], BF16) for i in range(4)]
        self.mergedb = fw.sbuf("mergedb", [128, 8, LT], BF16)
        self.qkpre = fw.sbuf("qkpre", [128, 4, 516])
        self.vaug = fw.sbuf("vaug", [128, 4, 512])
        self.vtokh = fw.sbuf("vtokh", [128, 4, 256])
        self.ar = [fw.sbuf("ar%d" % i, [128, 2, LT]) for i in range(2)]
        self.art = [fw.sbuf("art%d" % i, [128, 2, LT]) for i in range(2)]
        self.s5ring = [fw.sbuf("s5r%d" % i, [128, S5W]) for i in range(2)]
        self.s5i = 0
        self.s5ringb = [fw.sbuf("s5rb%d" % i, [128, 512], BF16) for i in range(2)]
        self.sqb = [fw.sbuf("sqb%d" % i, [128, LT], BF16) for i in range(2)]
        self.cvb = [fw.sbuf("cvb%d" % i, [128, LT], BF16) for i in range(2)]
        self.bg = None
        self.sqi = 0
        self.onesb = fw.sbuf("onesb", [128, 128], BF16)
        self.bob = fw.sbuf("bob", [128, 128], BF16)
        self.stm = [fw.sbuf("stm%d" % i, [128, 128]) for i in range(4)]
        self.sth = [fw.sbuf("sth%d" % i, [128, 64]) for i in range(4)]
        self.strw = [fw.sbuf("str%d" % i, [128, 64]) for i in range(4)]
        self.kz = [fw.sbuf("kz%d" % i, [128, LT]) for i in range(2)]
        self.bz = [fw.sbuf("bz%d" % i, [128, LT]) for i in range(2)]
        self.s5carry = fw.sbuf("s5carry", [128, G])
        self.s5rho = fw.sbuf("s5rho", [128, 2 * G])
        self.rhalo = fw.sbuf("rhalo", [128, 8])
        self.fhalo = fw.sbuf("fhalo", [128, 44, 2])
        self.small = fw.sbuf("small", [128, 64])
        self.prt = [fw.sbuf("prt%d" % i, [128, LT], F32R) for i in range(4)]
        rem = nc.sbuf_bytes_remaining
        NW = (rem - 1024) // 2048
        self.NW = NW
        self.pool_t = [fw.sbuf("w%d" % i, [128, LT]) for i in range(NW)]
        self.free_w = list(self.pool_t)
        self.ps_t = [fw.psum("ps%d" % i, [128, LT]) for i in range(8)]
        self.free_ps = list(self.ps_t)
        self.dbg = {}
        self.debug = debug

    def wt(self):
        assert self.free_w, "work pool exhausted"
        return self.free_w.pop(0)

    def fr(self, *ts):
        for t in ts:
            assert t not in self.free_w
            self.free_w.append(t)

    def ps(self):
        assert self.free_ps, "psum pool exhausted"
        return self.free_ps.pop(0)

    def pf(self, *ts):
        for t in ts:
            self.free_ps.append(t)

    def cc(self, col, n=1, rows=slice(0, 128)):
        return self.consts[rows, col:col + n]

    def vv(self, l, name, j=0, rows=slice(0, 128)):
        c = l * self.VL.n + self.VL.off[name] + j
        return self.vecs[rows, c:c + 1]

    def mm(self, out, lhsT, rhs, start=True, stop=True):
        self.fw.op("pe", lambda h: h.matmul(out.ap, lhsT.ap, rhs.ap, start=start, stop=stop),
                   reads=[lhsT, rhs], writes=[out])

    def transpose(self, out, in_):
        n = in_.ap.shape[0]
        ident = self.consts[0:n, C_ID:C_ID + n]
        self.fw.op("pe", lambda h: h.transpose(out.ap, in_.ap, ident.ap), reads=[in_, ident], writes=[out])

    def act(self, out, in_, func, bias=None, scale=1.0, eng="act"):
        kw = {}
        reads = [in_]
        if bias is not None:
            if isinstance(bias, View):
                kw["bias"] = bias.ap
                reads.append(bias)
            else:
                kw["bias"] = bias
        if isinstance(scale, View):
            kw["scale"] = scale.ap
            reads.append(scale)
        elif scale != 1.0:
            kw["scale"] = scale
        self.fw.op("act", lambda h: h.activation(out.ap, in_.ap, func, **kw), reads=reads, writes=[out])

    def tt(self, out, in0, in1, op, eng="dve"):
        self.fw.op(eng, lambda h: h.tensor_tensor(out.ap, in0.ap, in1.ap, op), reads=[in0, in1], writes=[out])

    def ts(self, out, in0, s1, s2=None, op0=ALU.mult, op1=None, eng="dve"):
        reads = [in0]
        a1 = s1
        if isinstance(s1, View):
            a1 = s1.ap
            reads.append(s1)
        a2 = s2
        if isinstance(s2, View):
            a2 = s2.ap
            reads.append(s2)
        if op1 is None:
            self.fw.op(eng, lambda h: h.tensor_scalar(out.ap, in0.ap, a1, None, op0), reads=reads, writes=[out])
        else:
            self.fw.op(eng, lambda h: h.tensor_scalar(out.ap, in0.ap, a1, a2, op0, op1), reads=reads, writes=[out])

    def stt(self, out, in0, scalar, in1, op0, op1):
        reads = [in0, in1]
        a = scalar
        if isinstance(scalar, View):
            a = scalar.ap
            reads.append(scalar)
        self.fw.op("dve", lambda h: h.scalar_tensor_tensor(out.ap, in0.ap, a, in1.ap, op0, op1), reads=reads, writes=[out])

    def copy(self, out, in_, eng="act"):
        if eng == "act":
            self.fw.op("act", lambda h: h.copy(out.ap, in_.ap), reads=[in_], writes=[out])
        else:
            self.fw.op(eng, lambda h: h.tensor_copy(out.ap, in_.ap), reads=[in_], writes=[out])

    def scan(self, out, d0, d1, init, op0=ALU.mult, op1=ALU.add):
        reads = [d0, d1]
        a = init
        if isinstance(init, View):
            a = init.ap
            reads.append(init)
        self.fw.op("dve", lambda h: h.tensor_tensor_scan(out.ap, d0.ap, d1.ap, a, op0, op1), reads=reads, writes=[out])

    def recip(self, out, in_):
        self.fw.op("dve", lambda h: h.reciprocal(out.ap, in_.ap), reads=[in_], writes=[out])

    def memset(self, out, val, eng="pool"):
        self.fw.op(eng, lambda h: h.memset(out.ap, val), writes=[out])

    def load(self, out, in_, q="sp", **kw):
        self.fw.dma(q, out, in_, **kw)

    def dump(self, name, view, shape):
        if not self.debug:
            return
        d = self.fw.dram("dbg_" + name, list(shape), kind="ExternalOutput")
        self.dbg[name] = d
        self.fw.dma("pool", d[:], view)

    def wload(self, l, bi):
        slot = self.wring[self.wi % len(self.wring)]
        self.wi += 1
        self.load(slot[:], self.wbf[l][bi])
        return slot

    def dense_fm(self, slot, rhs, ms, consume):
        for m in ms:
            ps = self.ps()
            for k in range(8):
                self.mm(ps[:, :], slot[:, k * 512 + m * 128:k * 512 + (m + 1) * 128], rhs[:, k, :],
                        start=(k == 0), stop=(k == 7))
            consume(m, ps)
            self.pf(ps)

    def sq_tile(self):
        t = self.sqb[self.sqi % 2]
        self.sqi += 1
        return t

    def rstd_from(self, ps_ss, inv_n, eps_col):
        r = self.wt()
        self.act(r[:, :], ps_ss[:, :], AF.Sqrt, bias=self.cc(eps_col), scale=inv_n)
        self.recip(r[:, :], r[:, :])
        return r

    def rmsnorm(self, src, l, nidx, dst):
        ps = self.ps()
        for k in range(8):
            sq = self.sq_tile()
            self.act(sq[:, :], src(k), AF.Square)
            self.mm(ps[:, :], self.onesb[:, :], sq[:, :], start=(k == 0), stop=(k == 7))
        r = self.rstd_from(ps, 1.0 / D, C_EPSR)
        self.pf(ps)
        for k in range(8):
            self.stt(dst(k), src(k), self.vv(l, "norms", nidx * 8 + k), r[:, :], ALU.mult, ALU.mult)
        self.fr(r)

    def headnorm_rstd(self, hm, eps_col):
        sq = self.sq_tile()
        self.act(sq[:, :], hm[:, :], AF.Square)
        ps = self.ps()
        self.mm(ps[:, :], self.bob[:, :], sq[:, :])
        r = self.rstd_from(ps, 1.0 / 64, eps_col)
        self.pf(ps)
        return r

    def prologue(self):
        L = self.L
        self.load(self.consts[:, :], self.consts_d[:, :])
        self.load(self.vecs[:, :], self.vecs_d[:, :])
        self.copy(self.onesb[:, :], self.cc(C_ONES, 128), eng="dve")
        self.copy(self.bob[:, :], self.cc(C_BO, 128), eng="dve")
        hl = self.small
        self.load(hl[:, 0:2 * L], self.hlb_d[:, :])
        e = self.wt()
        self.act(e[:, 0:2 * L], hl[:, 0:2 * L], AF.Exp)
        for c in range(2):
            ssum = e[:, 32 + c:33 + c]
            self.fw.op("dve", lambda h, c=c, ssum=ssum: h.tensor_reduce(ssum.ap, e[:, c * L:(c + 1) * L].ap, mybir.AxisListType.X, ALU.add),
                       reads=[e], writes=[e])
            self.recip(ssum, ssum)
            self.ts(e[:, c * L:(c + 1) * L], e[:, c * L:(c + 1) * L], ssum, None, ALU.mult)
            self.memset(self.lbs[:, c * L:c * L + 1], 0.0, eng="dve")
            for l in range(1, L):
                self.tt(self.lbs[:, c * L + l:c * L + l + 1], self.lbs[:, c * L + l - 1:c * L + l], e[:, c * L + l:c * L + l + 1], ALU.add)
        self.ts(self.lbs[:, 2 * L:4 * L], self.lbs[:, 0:2 * L], -1.0, 1.0, ALU.mult, ALU.add)
        self.ts(self.lbs[:, 4 * L:6 * L], self.lbs[:, 0:2 * L], 1.0, -1.0, ALU.mult, ALU.add)
        self.fr(e)
        nst = min(3, self.NW // 8)
        stg = [self.pool_t[8 * i:8 * i + 8] for i in range(nst)]
        engs = ["act", "dve", "pool"]
        for b in range(NBLK):
            st = stg[b % nst]
            for j in range(8):
                self.load(st[j][:, :], self.wpack[b][:, j * 512:(j + 1) * 512])
            slot = self.wring[b % 2]
            for j in range(8):
                self.copy(slot[:, j * 512:(j + 1) * 512], st[j][:, :], eng=engs[(b * 8 + j) % 3])
            self.fw.dma("pool", self.wbf[0][b], slot[:, :])
        self.memset(self.vaug[:, :, :], 1.0, eng="dve")
        for t in self.kz + self.bz:
            self.memset(t[:, :], 0.0, eng="dve")

    def layer_prep(self, l):
        self.load(self.mats[:, :], self.mats_d[l])
        for t in self.stm + self.sth + self.strw:
            self.memset(t[:, :], 0.0, eng="dve")
        self.memset(self.s5carry[:, :], 0.0, eng="dve")
        self.memset(self.rhalo[:, :], 0.0, eng="dve")
        self.memset(self.fhalo[:, :, :], 0.0, eng="dve")
        self.memset(self.qkpre[:, :, 0:3], 0.0, eng="dve")
        if "s" in self.mixers:
            self.s5_prep(l)

    def range_reduce(self, x, tmp, tmpi):
        C1 = 6.28125
        C2 = TWO_PI - C1
        self.ts(tmp, x, 1.0 / TWO_PI, 0.5, ALU.mult, ALU.add)
        self.copy(tmpi, tmp, eng="dve")
        self.copy(tmp, tmpi, eng="dve")
        self.stt(x, tmp, -C1, x, ALU.mult, ALU.add)
        self.stt(x, tmp, -C2, x, ALU.mult, ALU.add)
        self.ts(tmp, x, math.pi, -TWO_PI, ALU.is_gt, ALU.mult)
        self.tt(x, x, tmp, ALU.add)
        self.ts(tmp, x, -math.pi, TWO_PI, ALU.is_lt, ALU.mult)
        self.tt(x, x, tmp, ALU.add)
        self.ts(x, x, math.pi, -math.pi, ALU.min, ALU.max)

    def s5_prep(self, l):
        are, aim, stp, t1, t2, t3, t4, t5 = [self.wt() for _ in range(8)]
        A = lambda t: t[:, :].bc([128, 1024]) if False else t
        cre = [self.wt(), self.wt()]
        cim = [self.wt(), self.wt()]
        ti = self.wt()
        for hf in range(2):
            sl = slice(hf * 512, (hf + 1) * 512)
            self.load(are[:, :], self.s5row_d[l, 0:1, sl].bc([128, 512]))
            self.load(aim[:, :], self.s5row_d[l, 1:2, sl].bc([128, 512]))
            self.load(stp[:, :], self.s5row_d[l, 2:3, sl].bc([128, 512]))
            self.act(stp[:, :], stp[:, :], AF.Exp)
            self.tt(t1[:, :], are[:, :], stp[:, :], ALU.mult)
            self.act(t1[:, :], t1[:, :], AF.Exp)
            self.tt(t2[:, :], aim[:, :], stp[:, :], ALU.mult)
            self.ts(t3[:, :], t2[:, :], math.pi / 2, None, ALU.add)
            self.range_reduce(t2[:, :], t4[:, :], ti[:, :].m(lambda a: a.bitcast(I32)))
            self.range_reduce(t3[:, :], t4[:, :], ti[:, :].m(lambda a: a.bitcast(I32)))
            self.act(t2[:, :], t2[:, :], AF.Sin)
            self.act(t3[:, :], t3[:, :], AF.Sin)
            self.tt(t2[:, :], t2[:, :], t1[:, :], ALU.mult)
            self.tt(t3[:, :], t3[:, :], t1[:, :], ALU.mult)
            self.ts(t3[:, :], t3[:, :], -1.0, None, ALU.add)
            self.tt(t4[:, :], are[:, :], are[:, :], ALU.mult)
            self.tt(t5[:, :], aim[:, :], aim[:, :], ALU.mult)
            self.tt(t4[:, :], t4[:, :], t5[:, :], ALU.add)
            self.recip(t4[:, :], t4[:, :])
            self.tt(t5[:, :], t3[:, :], are[:, :], ALU.mult)
            self.tt(t1[:, :], t2[:, :], aim[:, :], ALU.mult)
            self.tt(t5[:, :], t5[:, :], t1[:, :], ALU.add)
            self.tt(cre[hf][:, :], t5[:, :], t4[:, :], ALU.mult)
            self.tt(t5[:, :], t2[:, :], are[:, :], ALU.mult)
            self.tt(t1[:, :], t3[:, :], aim[:, :], ALU.mult)
            self.tt(t5[:, :], t5[:, :], t1[:, :], ALU.subtract)
            self.tt(cim[hf][:, :], t5[:, :], t4[:, :], ALU.mult)
        bre, bim = are, aim
        for hf in range(2):
            self.load(bre[:, :], self.s5b_d[l][:, hf * 512:(hf + 1) * 512])
            self.load(bim[:, :], self.s5b_d[l][:, 1024 + hf * 512:1024 + (hf + 1) * 512])
            self.tt(t1[:, :], cre[hf][:, :], bre[:, :], ALU.mult)
            self.tt(t2[:, :], cim[hf][:, :], bim[:, :], ALU.mult)
            self.tt(t1[:, :], t1[:, :], t2[:, :], ALU.subtract)
            self.tt(t2[:, :], cre[hf][:, :], bim[:, :], ALU.mult)
            self.tt(t3[:, :], cim[hf][:, :], bre[:, :], ALU.mult)
            self.tt(t2[:, :], t2[:, :], t3[:, :], ALU.add)
            self.ts(t3[:, :], t1[:, :], -1.0, None, ALU.mult)
            for gg in range(8):
                g = hf * 8 + gg
                sl = slice(gg * 64, (gg + 1) * 64)
                self.fw.dma("pool", self.s5s[g][:, 0:64], t1[:, sl])
                self.fw.dma("pool", self.s5s[g][:, 64:128], t2[:, sl])
                self.fw.dma("pool", self.s5s[g][:, 128:192], t2[:, sl])
                self.fw.dma("pool", self.s5s[g][:, 192:256], t3[:, sl])
        for g in range(G):
            self.fw.dma("pool", self.s5s[g][:, 256:512], self.s5c_d[l, g])
        for g in range(G):
            self.load(t1[:, :], self.s5s[g][:, 0:512])
            tb = t2[:, :].m(lambda ap: ap.bitcast(BF16))[:, 0:512]
            self.copy(tb, t1[:, :], eng="pool")
            self.fw.dma("pool", self.s5sb[g], tb)
        sc = self.small
        self.load(sc[:, 0:3 * G], self.s5col_d[l])
        self.act(sc[:, 2 * G:3 * G], sc[:, 2 * G:3 * G], AF.Exp)
        self.tt(self.s5rho[:, 0:G], sc[:, G:2 * G], sc[:, 2 * G:3 * G], ALU.mult)
        self.tt(self.s5rho[:, G:2 * G], sc[:, 0:G], sc[:, 2 * G:3 * G], ALU.mult)
        self.act(self.s5rho[:, G:2 * G], self.s5rho[:, G:2 * G], AF.Exp)
        tau = self.cc(C_TAU, 512)
        for g in range(G):
            th = self.s5rho[:, g:g + 1]
            self.ts(t1[:, :], tau, th, None, ALU.mult)
            self.ts(t2[:, :], t1[:, :], math.pi / 2, None, ALU.add)
            self.range_reduce(t1[:, :], t4[:, :], ti[:, :].m(lambda a: a.bitcast(I32)))
            self.range_reduce(t2[:, :], t4[:, :], ti[:, :].m(lambda a: a.bitcast(I32)))
            self.act(t1[:, :], t1[:, :], AF.Sin)
            self.act(t2[:, :], t2[:, :], AF.Sin)
            self.ts(t3[:, 0:128], self.cc(C_ID, 128), t2[:, 511:512], None, ALU.mult)
            self.ts(t3[:, 128:129], t1[:, 511:512], self.cc(C_SGN), None, ALU.mult)
            self.stt(t3[:, 0:128], self.cc(C_SWAP, 128), t3[:, 128:129], t3[:, 0:128], ALU.mult, ALU.add)
            self.fw.dma("pool", self.s5s[g][:, 512:1024], t2[:, :])
            self.fw.dma("pool", self.s5s[g][:, 1024:1536], t1[:, :])
            self.fw.dma("pool", self.s5s[g][:, 1536:1664], t3[:, 0:128])
        self.fr(are, aim, stp, t1, t2, t3, t4, t5, ti, *cre, *cim)

    def cla(self, Qt, Qh, Kt, kz, vnum, vden, vst, STz, nst, dec, ps_num, ps_den, clamp=False):
        for blk in self.blks:
            tok = slice(blk * 128, (blk + 1) * 128)
            for hh in range(2):
                hp = slice(64 * hh, 64 * hh + 64)
                pp = self.ps()
                self.mm(pp[:, 0:128], Kt[hp, tok], Qt[hp, tok])
                pm = self.wt()
                if clamp:
                    self.ts(pm[:, 0:128], pp[:, 0:128], 1e30, -1e30, ALU.min, ALU.max)
                    self.tt(pm[:, 0:128], pm[:, 0:128], self.cc(C_MU, 128), ALU.mult)
                else:
                    self.tt(pm[:, 0:128], pp[:, 0:128], self.cc(C_MU, 128), ALU.mult)
                self.pf(pp)
                CS = 9
                if CS >= 2:
                    self.mm(ps_num[hp, tok], vnum(blk, hh), pm[:, 0:128], start=True, stop=(CS < 3))
                else:
                    self.mm(ps_num[hp, tok], self.cc(C_BO + 64 * hh, 64), pm[:, 0:128], start=True, stop=True)
                if CS >= 3:
                    self.mm(ps_num[hp, tok], STz[hh][:, 0:64], Qh[:, tok], start=False, stop=True)
                if vden is not None:
                    self.mm(ps_den[hp, tok], vden(blk, hh), pm[:, 0:128], start=True, stop=False)
                    self.mm(ps_den[hp, tok], STz[hh][:, 64:128], Qh[:, tok], start=False, stop=True)
                if CS >= 4:
                    pst = self.ps()
                    self.mm(pst[:, 0:nst], kz[hh][:, tok], vst(blk, hh))
                    self.stt(STz[hh][:, 0:nst], STz[hh][:, 0:nst], dec[:, blk:blk + 1], pst[:, 0:nst], ALU.mult, ALU.add)
                    self.pf(pst)
                self.fr(pm)
                yield "CHAIN"

    def to_tok(self, src, dst):
        for blk in self.blks:
            pt = self.ps()
            self.transpose(pt[:, 0:128], src[:, blk * 128:(blk + 1) * 128])
            self.copy(dst[:, blk * 128:(blk + 1) * 128], pt[:, 0:128])
            self.pf(pt)

    def to_tok_z(self, src, dz):
        for blk in self.blks:
            pt = self.ps()
            self.transpose(pt[:, 0:128], src[:, blk * 128:(blk + 1) * 128])
            self.copy(dz[0][:, blk * 128:blk * 128 + 64], pt[:, 0:64])
            self.copy(dz[1][:, blk * 128 + 64:blk * 128 + 128], pt[:, 64:128])
            self.pf(pt)

    def mlstm(self, l, i):
        slot = self.wload(l, 0)
        self.dense_fm(slot, self.ub, range(4), lambda m, ps: self.copy(self.qkpre[:, m, 3:515], ps[:, :]))
        qk = []
        for m in range(4):
            acc = self.wt()
            self.ts(acc[:, :], self.qkpre[:, m, 0:512], self.vv(l, "mconvw", m * 4 + 0), self.vv(l, "mconvb", m), ALU.mult, ALU.add)
            for j in range(1, 4):
                self.stt(acc[:, :], self.qkpre[:, m, j:j + 512], self.vv(l, "mconvw", m * 4 + j), acc[:, :], ALU.mult, ALU.add)
            self.act(acc[:, :], acc[:, :], AF.Silu)
            qk.append(acc)
            self.copy(self.qkpre[:, m, 0:3], self.qkpre[:, m, 512:515], eng="pool")
        slot = self.wload(l, 2)
        li, lf = [None, None], [None, None]

        def gcons(m, ps):
            t = self.wt()
            if m < 2:
                self.act(t[:, :], ps[:, :], AF.Identity, bias=self.vv(l, "mig", m))
                li[m] = t
            else:
                self.act(t[:, :], ps[:, :], AF.Sigmoid, bias=self.vv(l, "mfg", m - 2))
                self.act(t[:, :], t[:, :], AF.Ln)
                lf[m - 2] = t
        self.dense_fm(slot, self.ub, range(4), gcons)
        slot = self.wload(l, 7)
        for blk in self.blks:
            ps = self.ps()
            for k in range(8):
                self.mm(ps[:, :], self.ub[:, k, blk * 128:(blk + 1) * 128], slot[:, k * 512:(k + 1) * 512], start=(k == 0), stop=(k == 7))
            self.copy(self.vaug[:, blk, :].re("p (h e) -> p h e", e=128)[:, :, 0:64], ps[:, 0:256].re("p (h e) -> p h e", e=64))
            self.copy(self.vtokh[:, blk, :], ps[:, 256:512])
            self.pf(ps)
        slot = self.wload(l, 1)
        og, hg = [None, None], [None, None]

        def ocons(m, ps):
            t = self.wt()
            if m < 2:
                self.act(t[:, :], ps[:, :], AF.Sigmoid)
                og[m] = t
            else:
                self.act(t[:, :], ps[:, :], AF.Silu)
                hg[m - 2] = t
        self.dense_fm(slot, self.ub, range(4), ocons)
        self.hg = hg
        if "m" not in self.mixers:
            self.fr(*qk, *li, *lf, *og)
            self.memset(self.ybf[0][:, :, :], 0.0)
            return
        for c in range(2):
            q, k = qk[c], qk[2 + c]
            b = lf[c]
            self.scan(b[:, :], self.cc(C_RST, 512), lf[c][:, :], 0.0)
            f1 = li[c]
            self.tt(f1[:, :], li[c][:, :], b[:, :], ALU.subtract)
            self.act(f1[:, :], f1[:, :], AF.Exp)
            eb = b
            self.act(eb[:, :], b[:, :], AF.Exp)
            self.stt(q[:, :], q[:, :], 0.125, eb[:, :], ALU.mult, ALU.mult)
            self.tt(k[:, :], k[:, :], f1[:, :], ALU.mult)
            if i == 0:
                self.memset(k[:, 0:PADF], 0.0, eng="dve")
            kh = f1
            eb_last = eb[:, :].re("p (c t) -> p c t", t=128)[:, :, 127:128]
            self.tt(kh[:, :].re("p (c t) -> p c t", t=128), k[:, :].re("p (c t) -> p c t", t=128), eb_last.bc([128, 4, 128]), ALU.mult)
            self.to_tok_z(kh, self.kz)
            self.fr(kh)
            ps_num, ps_den = self.ps(), self.ps()
            dec = eb[:, :].re("p (c t) -> p c t", t=128)[:, :, 127]
            vn = lambda blk, hh, c=c: self.vaug[:, blk, (2 * c + hh) * 128:(2 * c + hh) * 128 + 64]
            vd = lambda blk, hh, c=c: self.vaug[:, blk, (2 * c + hh) * 128 + 64:(2 * c + hh) * 128 + 128]
            vs = lambda blk, hh, c=c: self.vaug[:, blk, (2 * c + hh) * 128:(2 * c + hh) * 128 + 128]
            yield from self.cla(q, q, k, self.kz, vn, vd, vs, self.stm[2 * c:2 * c + 2], 128, dec, ps_num, ps_den)
            self.fr(eb, k)
            a = q
            self.act(a[:, :], ps_den[:, :], AF.Abs)
            self.ts(a[:, :], a[:, :], 1.0, None, ALU.max)
            self.recip(a[:, :], a[:, :])
            hm = self.wt()
            self.tt(hm[:, :], ps_num[:, :], a[:, :], ALU.mult)
            self.pf(ps_num, ps_den)
            r = self.headnorm_rstd(hm, C_EPSR)
            self.stt(hm[:, :], hm[:, :], self.vv(l, "mnorm", c), r[:, :], ALU.mult, ALU.mult)
            self.tt(self.ybf[0][:, c, :], hm[:, :], og[c][:, :], ALU.mult)
            self.fr(a, hm, r, og[c])

    def hgrn2(self, l, i):
        L = self.L
        slot = self.wload(l, 5)
        qs, kk_, lfs = [None, None], [None, None], [None, None]

        def cons(m, ps):
            if m < 2:
                t = self.wt()
                self.act(t[:, :], ps[:, :], AF.Silu)
                qs[m] = t
            else:
                c = m - 2
                sg, lf_, k = self.wt(), self.wt(), self.wt()
                self.act(sg[:, :], ps[:, :], AF.Sigmoid)
                self.ts(lf_[:, :], sg[:, :], self.lbs[:, 2 * L + c * L + l:2 * L + c * L + l + 1], self.lbs[:, c * L + l:c * L + l + 1], ALU.mult, ALU.add)
                self.act(lf_[:, :], lf_[:, :], AF.Ln)
                self.ts(k[:, :], sg[:, :], self.lbs[:, 4 * L + c * L + l:4 * L + c * L + l + 1], self.lbs[:, 2 * L + c * L + l:2 * L + c * L + l + 1], ALU.mult, ALU.add)
                self.fr(sg)
                kk_[c], lfs[c] = k, lf_
        self.dense_fm(slot, self.ub, range(4), cons)
        hg = self.hg
        yield "PREPDONE"
        self.fw.tag = None
        if "h" not in self.mixers:
            self.fr(*qs, *kk_, *lfs, *hg)
            self.memset(self.ybf[2][:, :, :], 0.0)
            return
        for c in range(2):
            g = lfs[c]
            self.scan(g[:, :], self.cc(C_RST, 512), lfs[c][:, :], 0.0)
            g3 = g[:, :].re("p (c t) -> p c t", t=128)
            r3 = lambda t: t[:, :].re("p (c t) -> p c t", t=128)
            ein, gd, qt, kt, kh = self.wt(), self.wt(), self.wt(), self.wt(), self.wt()
            self.act(ein[:, :], g[:, :], AF.Exp)
            self.tt(r3(gd), g3, g3[:, :, 63:64].bc([128, 4, 128]), ALU.subtract)
            self.act(qt[:, :], gd[:, :], AF.Exp)
            self.tt(qt[:, :], qt[:, :], qs[c][:, :], ALU.mult)
            self.act(kt[:, :], gd[:, :], AF.Exp, scale=-1.0)
            self.tt(kt[:, :], kt[:, :], kk_[c][:, :], ALU.mult)
            self.tt(r3(gd), g3[:, :, 127:128].bc([128, 4, 128]), g3, ALU.subtract)
            self.act(kh[:, :], gd[:, :], AF.Exp)
            self.tt(kh[:, :], kh[:, :], kk_[c][:, :], ALU.mult)
            qh = qs[c]
            self.tt(qh[:, :], qs[c][:, :], ein[:, :], ALU.mult)
            self.to_tok_z(kh, self.kz)
            self.fr(kh, kk_[c], g, gd)
            ps_num = self.ps()
            dec = ein[:, :].re("p (c t) -> p c t", t=128)[:, :, 127]
            vn = lambda blk, hh, c=c: self.vtokh[:, blk, (2 * c + hh) * 64:(2 * c + hh) * 64 + 64]
            yield from self.cla(qt, qh, kt, self.kz, vn, None, vn, self.sth[2 * c:2 * c + 2], 64, dec, ps_num, None, clamp=True)
            self.fr(ein, qt, kt)
            hm = qh
            self.copy(hm[:, :], ps_num[:, :])
            self.pf(ps_num)
            r = self.headnorm_rstd(hm, C_EPSR)
            self.stt(hm[:, :], hm[:, :], self.vv(l, "hnorm", c), r[:, :], ALU.mult, ALU.mult)
            self.tt(self.ybf[2][:, c, :], hm[:, :], hg[c][:, :], ALU.mult)
            self.fr(hm, r, hg[c])

    def s5(self, l, i):
        slot = self.wload(l, 6)
        us = [None, None]

        def cons(m, ps):
            t = self.wt()
            self.copy(t[:, :], ps[:, :])
            self.copy(self.ybf[3][:, m, :], ps[:, :])
            us[m] = t
        self.dense_fm(slot, self.ub, range(2), cons)
        yield "PREPDONE"
        if "s" not in self.mixers:
            self.fr(*us)
            self.memset(self.ybf[3][:, :, :], 0.0)
            return
        yg = [None, None]
        for c in range(2):
            py = self.ps()
            nmm = [0]

            def group(gg, c=c, py=py, nmm=nmm):
                g = c * 8 + gg
                sl = self.s5ring[self.s5i % 2]
                slb = self.s5ringb[self.s5i % 2]
                self.s5i += 1
                self.load(sl[:, 512:S5W_], self.s5s[g][:, 512:S5W_])
                self.load(slb[:, :], self.s5sb[g])
                pa, pb = self.ps(), self.ps()
                self.mm(pa[:, :], slb[:, 0:128], self.ybf[3][:, c, :])
                self.mm(pb[:, :], slb[:, 128:256], self.ybf[3][:, c, :])
                t1, t2 = self.wt(), self.wt()
                self.tt(t1[:, :], pa[:, :], sl[:, 512:1024], ALU.mult)
                self.tt(t2[:, :], pb[:, :], sl[:, 1024:1536], ALU.mult)
                self.pf(pa, pb)
                yield
                self.tt(t1[:, :], t1[:, :], t2[:, :], ALU.add, eng="pool")
                z = t2
                self.scan(z[:, :], self.s5rho[:, G + g:G + g + 1].bc([128, 512]), t1[:, :], self.s5carry[:, g:g + 1])
                yield
                zs_t = self.wt()
                zc = t1[:, :].m(lambda ap: ap.bitcast(BF16))[:, 0:512]
                zs = zs_t[:, :].m(lambda ap: ap.bitcast(BF16))[:, 0:512]
                self.stt(zc, z[:, :], self.cc(C_SGN), sl[:, 512:1024], ALU.mult, ALU.mult)
                self.stt(zs, z[:, :], -1.0, sl[:, 1024:1536], ALU.mult, ALU.mult)
                yield
                self.mm(py[:, :], slb[:, 256:384], zc, start=(nmm[0] == 0), stop=False)
                self.mm(py[:, :], slb[:, 384:512], zs, start=False, stop=(nmm[0] == 14))
                nmm[0] += 2
                pc = self.ps()
                self.mm(pc[:, 0:1], sl[:, 1536:1664], z[:, 511:512])
                self.copy(self.s5carry[:, g:g + 1], pc[:, 0:1])
                self.pf(pc)
                self.fr(t1, t2, zs_t)
            pend = list(range(8))
            act_ = []
            while pend or act_:
                while pend and len(act_) < 2:
                    act_.append(group(pend.pop(0)))
                for th in list(act_):
                    try:
                        next(th)
                    except StopIteration:
                        act_.remove(th)
                yield "CHAIN"
            y = self.wt()
            self.stt(y[:, :], us[c][:, :], self.vv(l, "sd", c), py[:, :], ALU.mult, ALU.add)
            self.pf(py)
            self.act(y[:, :], y[:, :], AF.Gelu_apprx_tanh)
            yg[c] = y
        self.fr(*us)
        for c in range(2):
            ps = self.ps()
            for kk in range(2):
                self.mm(ps[:, :], self.mats[:, MAT_GLU + kk * 256 + c * 128:MAT_GLU + kk * 256 + (c + 1) * 128], yg[kk][:, :],
                        start=(kk == 0), stop=(kk == 1))
            gt = self.wt()
            self.act(gt[:, :], ps[:, :], AF.Sigmoid, bias=self.vv(l, "sbglu", c))
            self.pf(ps)
            self.tt(self.ybf[3][:, c, :], yg[c][:, :], gt[:, :], ALU.mult)
            self.fr(gt)
        self.fr(*yg)

    def rwkv_prep(self, l, i):
        xm = [None] * 8
        self.xm = xm
        for bi in (3, 4):
            slot = self.wload(l, bi)
            for m in range(4):
                j = (bi - 3) * 4 + m
                ps = self.ps()
                for k in range(8):
                    self.mm(ps[:, :], slot[:, k * 512 + m * 128:k * 512 + (m + 1) * 128], self.ub[:, k, :], start=(k == 0), stop=(k == 7))
                cur, d = self.wt(), self.wt()
                self.copy(cur[:, :], ps[:, :])
                self.pf(ps)
                self.tt(d[:, 1:512], cur[:, 0:511], cur[:, 1:512], ALU.subtract)
                self.tt(d[:, 0:1], self.rhalo[:, j:j + 1], cur[:, 0:1], ALU.subtract)
                self.copy(self.rhalo[:, j:j + 1], cur[:, 511:512], eng="dve")
                self.stt(d[:, :], d[:, :], self.vv(l, "rmu", j), cur[:, :], ALU.mult, ALU.add)
                self.fr(cur)
                xm[j] = d
                yield

    def rwkv(self, l, i):
        xm = self.xm
        if "r" not in self.mixers:
            self.fr(*xm)
            self.memset(self.ybf[1][:, :, :], 0.0)
            return
        r_, k_, v_ = xm[0:2], xm[2:4], xm[4:6]
        x6, x7 = xm[6], xm[7]
        self.act(x6[0:64, :], x6[0:64, :], AF.Tanh)
        self.act(x7[:, :], x7[:, :], AF.Sigmoid)
        lw, a_, gout = [None, None], [None, None], [None, None]
        for c in range(2):
            ps = self.ps()
            self.mm(ps[:, :], self.mats[0:64, MAT_W2A2 + c * 128:MAT_W2A2 + (c + 1) * 128], x6[0:64, :])
            t = self.wt()
            self.act(t[:, :], ps[:, :], AF.Sigmoid, bias=self.vv(l, "rw0", c))
            self.ts(t[:, :], t[:, :], -0.6065306597126334, None, ALU.mult)
            lw[c] = t
            self.pf(ps)
            ps = self.ps()
            self.mm(ps[:, :], self.mats[64:128, MAT_W2A2 + c * 128:MAT_W2A2 + (c + 1) * 128], x6[64:128, :])
            t = self.wt()
            self.act(t[:, :], ps[:, :], AF.Sigmoid, bias=self.vv(l, "ra0", c))
            a_[c] = t
            self.pf(ps)
            ps = self.ps()
            self.mm(ps[:, :], self.mats[:, MAT_G2 + c * 128:MAT_G2 + (c + 1) * 128], x7[:, :])
            t = self.wt()
            self.copy(t[:, :], ps[:, :])
            gout[c] = t
            self.pf(ps)
        self.fr(x6, x7)
        tsl = slice(i * LT, (i + 1) * LT)
        if l == 0:
            for c in range(2):
                self.fw.dma("pool", self.vfirst[c * 128:(c + 1) * 128, tsl], v_[c][:, :])
        else:
            ps = self.ps()
            for kk in range(2):
                self.mm(ps[0:32, :], self.mats[:, MAT_V1 + kk * 32:MAT_V1 + (kk + 1) * 32], v_[kk][:, :], start=(kk == 0), stop=(kk == 1))
            t1 = self.wt()
            self.copy(t1[0:32, :], ps[0:32, :])
            self.pf(ps)
            for c in range(2):
                ps = self.ps()
                self.mm(ps[:, :], self.mats[0:32, MAT_V2 + c * 128:MAT_V2 + (c + 1) * 128], t1[0:32, :])
                s, vf = self.wt(), self.wt()
                self.act(s[:, :], ps[:, :], AF.Sigmoid, bias=self.vv(l, "rv0", c))
                self.pf(ps)
                self.load(vf[:, :], self.vfirst[c * 128:(c + 1) * 128, tsl])
                self.tt(vf[:, :], vf[:, :], v_[c][:, :], ALU.subtract)
                self.tt(vf[:, :], vf[:, :], s[:, :], ALU.mult)
                self.tt(v_[c][:, :], v_[c][:, :], vf[:, :], ALU.add)
                self.fr(s, vf)
            self.fr(t1)
        ypair = []
        for c in range(2):
            r, k, v, a = r_[c], k_[c], v_[c], a_[c]
            kk = self.wt()
            sq = self.wt()
            self.ts(kk[:, :], k[:, :], self.vv(l, "rkk", c), None, ALU.mult)
            self.act(sq[:, :], kk[:, :], AF.Square)
            ps = self.ps()
            self.mm(ps[:, :], self.cc(C_BO, 128), sq[:, :])
            self.act(sq[:, :], ps[:, :], AF.Sqrt)
            self.pf(ps)
            self.ts(sq[:, :], sq[:, :], 1e-12, None, ALU.max)
            self.recip(sq[:, :], sq[:, :])
            self.tt(kk[:, :], kk[:, :], sq[:, :], ALU.mult)
            km = sq
            self.ts(km[:, :], a[:, :], -1.0, self.vv(l, "rka", c), ALU.add, ALU.mult)
            self.stt(km[:, :], km[:, :], 1.0, k[:, :], ALU.add, ALU.mult)
            self.fr(k)
            bon = self.wt()
            self.stt(bon[:, :], r[:, :], self.vv(l, "rrk", c), km[:, :], ALU.mult, ALU.mult)
            ps = self.ps()
            self.mm(ps[:, :], self.cc(C_BO, 128), bon[:, :])
            self.tt(bon[:, :], ps[:, :], v[:, :], ALU.mult)
            self.pf(ps)
            r3 = lambda t: t[:, :].re("p (c t) -> p c t", t=128)
            gc, gx, e1, e2 = self.wt(), self.wt(), self.wt(), self.wt()
            self.scan(gc[:, :], self.cc(C_RST, 512), lw[c][:, :], 0.0)
            self.tt(gx[:, :], gc[:, :], lw[c][:, :], ALU.subtract)
            ar, art = self.ar[c], self.art[c]
            self.act(e1[:, :], gx[:, :], AF.Exp)
            self.stt(art[:, 0, :], kk[:, :], -1.0, e1[:, :], ALU.mult, ALU.mult)
            self.act(e1[:, :], gc[:, :], AF.Exp)
            self.tt(art[:, 1, :], r[:, :], e1[:, :], ALU.mult)
            ein = lw[c]
            self.copy(ein[:, :], e1[:, :], eng="pool")
            gmid = r3(gc)[:, :, 63:64].bc([128, 4, 128])
            self.tt(r3(gx), r3(gx), gmid, ALU.subtract)
            self.act(e1[:, :], gx[:, :], AF.Exp)
            self.stt(ar[:, 0, :], kk[:, :], -1.0, e1[:, :], ALU.mult, ALU.mult)
            self.tt(r3(e2), r3(gc), gmid, ALU.subtract)
            self.act(e1[:, :], e2[:, :], AF.Exp)
            self.tt(ar[:, 1, :], r[:, :], e1[:, :], ALU.mult)
            self.act(e1[:, :], e2[:, :], AF.Exp, scale=-1.0)
            bv, bt, kt = r, gx, e2
            self.tt(bv[:, :], kk[:, :], a[:, :], ALU.mult)
            self.tt(bt[:, :], bv[:, :], e1[:, :], ALU.mult)
            self.tt(kt[:, :], km[:, :], e1[:, :], ALU.mult)
            self.fr(kk, a)
            glast = r3(gc)[:, :, 127:128].bc([128, 4, 128])
            self.tt(r3(e1), glast, r3(gc), ALU.subtract)
            self.act(e1[:, :], e1[:, :], AF.Exp)
            bh, kh = bv, km
            self.tt(bh[:, :], bv[:, :], e1[:, :], ALU.mult)
            self.tt(kh[:, :], km[:, :], e1[:, :], ALU.mult)
            self.to_tok_z(bh, self.bz)
            self.to_tok_z(kh, self.kz)
            vtok = gc
            self.to_tok(v, vtok)
            self.fr(bh, kh, v, e1)
            py = self.ps()
            gam2 = ein[:, :].re("p (c t) -> p c t", t=128)[:, :, 127]
            done = {}
            consumed = [self.blks[0] - 1]

            def chainA(blk, hh):
                bsl = slice(blk * 128, (blk + 1) * 128)
                hp = slice(64 * hh, 64 * hh + 64)
                m4, T = self.wt(), self.wt()
                PRt = self.prt[(blk % 2) * 2 + hh]
                PR = PRt[:, :]
                PP = PRt[:, :].m(lambda ap: ap.bitcast(F32))
                Tb = T[:, 0:128]
                pa, pn = self.ps(), self.ps()
                self.mm(pa[:, 0:256], bt[hp, bsl], ar[hp, :, bsl])
                self.mm(pa[:, 256:512], kt[hp, bsl], ar[hp, :, bsl])
                self.mm(pn[:, 0:128], ar[hp, 0, bsl], bt[hp, bsl])
                self.tt(m4[:, :].re("p (a m) -> p a m", a=2), pa[:, :].re("p (a m) -> p a m", a=2),
                        self.cc(C_MSU, 256).m(lambda ap: ap.unsqueeze(1)).bc([128, 2, 256]), ALU.mult)
                self.tt(T[:, 256:384], pn[:, 0:128], self.cc(C_MSL, 128), ALU.mult)
                self.pf(pa, pn)
                yield
                self.copy(PR[:, 0:128], T[:, 256:384])
                self.copy(PR[:, 128:256], m4[:, 0:128], eng="dve")
                self.tt(Tb, m4[:, 0:128], self.cc(C_ID, 128), ALU.add, eng="pool")
                yield
                NIT = 6
                for it in range(NIT + 1):
                    cur = (it % 2) * 256
                    nxt = 256 - cur
                    if it < NIT:
                        pq = self.ps()
                        self.mm(pq[:, 0:128], PR[:, cur + 128:cur + 256], PR[:, cur:cur + 128])
                        if it < NIT - 1:
                            self.mm(pq[:, 128:256], PR[:, cur:cur + 128], PR[:, cur + 128:cur + 256])
                            self.copy(PR[:, nxt:nxt + 256], pq[:, 0:256])
                        else:
                            self.copy(PR[:, nxt:nxt + 128], pq[:, 0:128])
                        self.pf(pq)
                    if it >= 1:
                        pq = self.ps()
                        self.mm(pq[:, 0:128], PP[:, cur:cur + 128], Tb)
                        self.tt(Tb, Tb, pq[:, 0:128], ALU.add)
                        self.pf(pq)
                    yield
                done[(blk, hh)] = (m4, T)

            def phaseB():
                for blk in self.blks:
                    bsl = slice(blk * 128, (blk + 1) * 128)
                    while (blk, 0) not in done or (blk, 1) not in done:
                        yield
                    for hh in range(2):
                        hc = slice(blk * 128 + hh * 64, blk * 128 + hh * 64 + 64)
                        ST = self.strw[2 * c + hh]
                        m4, T = done[(blk, hh)]
                        pz = self.ps()
                        self.mm(pz[:, 0:64], m4[:, 256:384], vtok[:, hc], start=True, stop=False)
                        self.mm(pz[:, 0:64], art[:, 0, bsl], ST[:, :], start=False, stop=True)
                        self.copy(T[:, 128:192], pz[:, 0:64])
                        self.pf(pz)
                    yield
                    for hh in range(2):
                        m4, T = done[(blk, hh)]
                        pz = self.ps()
                        self.mm(pz[:, 0:64], T[:, 0:128], T[:, 128:192])
                        self.copy(T[:, 192:256], pz[:, 0:64])
                        self.pf(pz)
                    yield
                    for hh in range(2):
                        hp = slice(64 * hh, 64 * hh + 64)
                        hc = slice(blk * 128 + hh * 64, blk * 128 + hh * 64 + 64)
                        ST = self.strw[2 * c + hh]
                        m4, T = done[(blk, hh)]
                        self.mm(py[hp, bsl], ST[:, :], art[:, 1, bsl], start=True, stop=False)
                        self.mm(py[hp, bsl], T[:, 192:256], m4[:, 128:256], start=False, stop=False)
                        self.mm(py[hp, bsl], vtok[:, hc], m4[:, 384:512], start=False, stop=True)
                        pst = self.ps()
                        self.mm(pst[:, 0:64], self.bz[hh][:, bsl], T[:, 192:256], start=True, stop=False)
                        self.mm(pst[:, 0:64], self.kz[hh][:, bsl], vtok[:, hc], start=False, stop=True)
                        self.stt(ST[:, :], ST[:, :], gam2[:, blk:blk + 1], pst[:, 0:64], ALU.mult, ALU.add)
                        self.pf(pst)
                        self.fr(m4, T)
                    consumed[0] = blk
                    yield

            pending = [(blk, hh) for blk in self.blks for hh in range(2)]
            threads = [phaseB()]
            nA = [0]

            def start_more():
                while pending and pending[0][0] <= consumed[0] + 2:
                    blk, hh = pending.pop(0)
                    threads.append(chainA(blk, hh))
            start_more()
            while threads:
                for th in list(threads):
                    try:
                        next(th)
                    except StopIteration:
                        threads.remove(th)
                start_more()
                yield "CHAIN"
            self.fr(bt, kt, vtok, ein)
            y = self.wt()
            self.copy(y[:, :], py[:, :])
            self.pf(py)
            ps = self.ps()
            self.mm(ps[:, :], self.cc(C_BO, 128), y[:, :])
            self.stt(y[:, :], ps[:, :], -1.0 / 64, y[:, :], ALU.mult, ALU.add)
            self.pf(ps)
            sq = self.wt()
            self.act(sq[:, :], y[:, :], AF.Square)
            ps = self.ps()
            self.mm(ps[:, :], self.cc(C_BO, 128), sq[:, :])
            self.act(sq[:, :], ps[:, :], AF.Sqrt, bias=self.cc(C_EPSG), scale=1.0 / 64)
            self.pf(ps)
            self.recip(sq[:, :], sq[:, :])
            self.tt(y[:, :], y[:, :], sq[:, :], ALU.mult)
            self.ts(y[:, :], y[:, :], self.vv(l, "rgnw", c), self.vv(l, "rgnb", c), ALU.mult, ALU.add)
            self.tt(y[:, :], y[:, :], bon[:, :], ALU.add)
            self.tt(self.ybf[1][:, c, :], y[:, :], gout[c][:, :], ALU.mult)
            self.fr(y, sq, bon, gout[c])

    def merge_branch(self, l, b):
        ga = self.wload(l, 8 + 2 * b)
        brs = self.wload(l, 16 + b // 2)
        for m in range(8):
            if m == 4:
                ga = self.wload(l, 9 + 2 * b)
            pg = self.ps()
            for k in range(8):
                self.mm(pg[:, :], ga[:, k * 512 + (m % 4) * 128:k * 512 + (m % 4 + 1) * 128], self.ub[:, k, :], start=(k == 0), stop=(k == 7))
            gt = self.wt()
            self.act(gt[:, :], pg[:, :], AF.Sigmoid, bias=self.vv(l, "bgate", b * 8 + m))
            self.pf(pg)
            pb = self.ps()
            for kk in range(2):
                sidx = (b % 2) * 4 + kk * 2 + m // 4
                j = m % 4
                self.mm(pb[:, :], brs[:, sidx * 512 + j * 128:sidx * 512 + (j + 1) * 128], self.ybf[b][:, kk, :], start=(kk == 0), stop=(kk == 1))
            if b == 0:
                self.tt(self.mergedb[:, m, :], gt[:, :], pb[:, :], ALU.mult)
            else:
                self.tt(gt[:, :], gt[:, :], pb[:, :], ALU.mult)
                self.tt(self.mergedb[:, m, :], self.mergedb[:, m, :], gt[:, :], ALU.add, eng="pool")
            self.pf(pb)
            self.fr(gt)
            yield

    def bg_convert(self, l):
        for b in range(NBLK):
            for j in range(8):
                t = self.wt()
                cb = self.cvb[(b * 8 + j) % 2]
                self.load(t[:, :], self.wpack[l * NBLK + b][:, j * 512:(j + 1) * 512])
                self.copy(cb[:, :], t[:, :], eng="pool")
                self.fr(t)
                self.fw.dma("pool", self.wbf[l][b][:, j * 512:(j + 1) * 512], cb[:, :])
                yield

    def bg_step(self, n=1):
        if self.bg is None:
            return
        for _ in range(n):
            try:
                next(self.bg)
            except StopIteration:
                self.bg = None
                return

    def with_side(self, main, side):
        side_live = side is not None
        for tok in main:
            if tok == "CHAIN":
                self.bg_step()
            if tok == "CHAIN" and side_live:
                try:
                    next(side)
                except StopIteration:
                    side_live = False
        if side_live:
            for _ in side:
                pass

    def interleave(self, mains, side):
        live = []
        for g in mains:
            for tok in g:
                if tok == "PREPDONE":
                    live.append(g)
                    break
        live.append(side)
        while live:
            self.bg_step()
            for g in list(live):
                try:
                    next(g)
                except StopIteration:
                    live.remove(g)

    def merge_out(self, l, i):
        self.proj_norm_res(l, [18, 19], lambda k: self.mergedb[:, k, :], 8, 1, i)

    def proj_norm_res(self, l, blocks, rhs_k, nk, nidx, i, ffn=False):
        mo = [None] * 8
        pss = self.ps()
        for c in range(2):
            if not ffn:
                slot = self.wload(l, blocks[c])
                pm = [self.ps() for _ in range(4)]
                for m in range(4):
                    for k in range(8):
                        self.mm(pm[m][:, :], slot[:, k * 512 + m * 128:k * 512 + (m + 1) * 128], rhs_k(k), start=(k == 0), stop=(k == 7))
            else:
                pm = [self.ps() for _ in range(4)]
                for g in range(3):
                    slot = self.wload(l, blocks[c * 3 + g])
                    for s in range(8):
                        k = 8 * g + s
                        if k >= 22:
                            continue
                        for m in range(4):
                            self.mm(pm[m][:, :], slot[:, s * 512 + m * 128:s * 512 + (m + 1) * 128], rhs_k(k), start=(k == 0), stop=(k == 21))
            for m in range(4):
                t = self.wt()
                self.copy(t[:, :], pm[m][:, :])
                sq = self.sq_tile()
                self.act(sq[:, :], pm[m][:, :], AF.Square)
                self.mm(pss[:, :], self.onesb[:, :], sq[:, :], start=(c == 0 and m == 0), stop=(c == 1 and m == 3))
                mo[c * 4 + m] = t
            self.pf(*pm)
        r = self.rstd_from(pss, 1.0 / D, C_EPSR)
        self.pf(pss)
        for k in range(8):
            self.stt(mo[k][:, :], mo[k][:, :], self.vv(l, "norms", nidx * 8 + k), r[:, :], ALU.mult, ALU.mult)
            self.tt(self.hT[:, k, :], self.hT[:, k, :], mo[k][:, :], ALU.add, eng="pool")
        self.fr(r, *mo)
        if i == 0:
            self.memset(self.hT[:, :, 0:PADF], 0.0, eng="pool")

    def ffn(self, l, i):
        self.rmsnorm(lambda k: self.hT[:, k, :], l, 2, lambda k: self.ub[:, k, :])
        acts = [self.wt() for _ in range(11)]
        actv = lambda j: acts[j // 2][:, :].m(lambda ap: ap.bitcast(BF16))[:, (j % 2) * 512:(j % 2 + 1) * 512]
        for bi in range(11):
            slot = self.wload(l, 20 + bi)
            conv = [None] * 4

            def cons(m, ps, bi=bi):
                ch = bi * 4 + m
                acc = self.wt()
                w = lambda j: self.vv(l, "fconvw", ch * 3 + j)
                self.act(acc[:, :], ps[:, :], AF.Identity, bias=self.vv(l, "fconvb", ch), scale=w(2))
                self.stt(acc[:, 1:512], ps[:, 0:511], w(1), acc[:, 1:512], ALU.mult, ALU.add)
                self.stt(acc[:, 2:512], ps[:, 0:510], w(0), acc[:, 2:512], ALU.mult, ALU.add)
                self.stt(acc[:, 0:1], self.fhalo[:, ch, 1:2], w(1), acc[:, 0:1], ALU.mult, ALU.add)
                self.stt(acc[:, 0:2], self.fhalo[:, ch, 0:2], w(0), acc[:, 0:2], ALU.mult, ALU.add)
                self.copy(self.fhalo[:, ch, :], ps[:, 510:512], eng="dve")
                conv[m] = acc
            self.dense_fm(slot, self.ub, range(4), cons)
            for m in range(2):
                self.act(conv[m][:, :], conv[m][:, :], AF.Gelu_apprx_tanh)
                self.tt(actv(2 * bi + m), conv[m][:, :], conv[2 + m][:, :], ALU.mult, eng="pool")
            self.fr(*conv)
        self.proj_norm_res(l, list(range(31, 37)), lambda k: actv(k), 22, 3, i, ffn=True)
        self.fr(*acts)

    def build(self):
        NT, L = self.NT, self.L
        self.gts = []
        self.prologue()
        for l in range(L):
            if self.bg is not None:
                for _ in self.bg:
                    pass
                self.bg = None
            if l + 1 < L:
                self.bg = self.bg_convert(l + 1)
            self.layer_prep(l)
            src = self.xT if l == 0 else (self.hA if l % 2 == 1 else self.hB)
            dst = self.out_d if l == L - 1 else (self.hA if l % 2 == 0 else self.hB)
            for i in range(NT):
                tsl = slice(i * LT, (i + 1) * LT)
                self.blks = [3] if i == 0 else [0, 1, 2, 3]
                self.load(self.hT[:, :, :], src[:, tsl].re("(k p) t -> p k t", p=128))
                self.rmsnorm(lambda k: self.hT[:, k, :], l, 0, lambda k: self.ub[:, k, :])
                self.with_side(self.mlstm(l, i), self.rwkv_prep(l, i))
                self.with_side(self.rwkv(l, i), self.merge_branch(l, 0))
                self.interleave([self.hgrn2(l, i), self.s5(l, i)], self.merge_branch(l, 1))
                self.with_side(self.merge_branch(l, 2), None)
                self.with_side(self.merge_branch(l, 3), None)
                self.merge_out(l, i)
                self.ffn(l, i)
                self.fw.dma("pool", dst[:, tsl].re("(k p) t -> p k t", p=128), self.hT[:, :, :])
        self.fw.wait_all("pool", [self.out_d])
        self.fw.emit()
        return self.nc


_CACHE = {}


def run(inputs, NT, L, ncores, mixers="mrhs", debug=False):
    hp = host_pack(inputs, L)
    x = np.asarray(inputs["x"], np.float32)
    meta = np.asarray(inputs["meta"], np.float32)
    key = (NT, L, mixers, debug)
    prog = Prog(NT, L, mixers, debug)
    nc = prog.build()
    in_maps = []
    for b in range(ncores):
        d = dict(hp)
        d["xT"] = host_x(x[b], meta, NT)
        in_maps.append(d)
    res = run_bass_kernel_spmd(nc, in_maps, core_ids=list(range(ncores)))
    outs = []
    for b in range(ncores):
        oT = np.asarray(res.results[b]["outT"])
        outs.append(np.ascontiguousarray(oT[:, LT:].T))
    return np.stack(outs, 0), res, prog


def kernel(**inputs):
    out, _, _ = run(inputs, 17, 4, 8)
    return out.astype(np.float32)
```
